# Optimizing a Trainium2 kernel written in Bass

```python
import math
import jax
import jax.numpy as jnp
from jax import lax
import numpy as np


D_MODEL = 1024
BATCH = 16
SEQ = 2048
DEPTH = 1
DEC_BATCH = 2
DEC_SEQ = 8192
PAST_LEN = 128

POOL_WINDOWS = (2, 4, 8, 16)
POOL_GROUPS = len(POOL_WINDOWS)
POOL_WIDTH = D_MODEL // 2
POOL_GROUP_DIM = POOL_WIDTH // POOL_GROUPS
N_HEADS = 8
QK_NOPE_DIM = 64
QK_ROPE_DIM = 32
QK_HEAD_DIM = QK_NOPE_DIM + QK_ROPE_DIM
V_HEAD_DIM = 64
Q_LORA_RANK = 256
KV_LORA_RANK = 128
ATTN_WIDTH = N_HEADS * V_HEAD_DIM
MIX_WIDTH = POOL_WIDTH + ATTN_WIDTH
IN_PROJ_WIDTH = POOL_WIDTH + Q_LORA_RANK + KV_LORA_RANK + QK_ROPE_DIM
Q_BLOCK = 128
ROPE_THETA = 10000.0
D_FF = 2816
CONV_WIDTH = 3
EPS = 1e-6

kernel_name = 'hymba_pool_mla_convffn_encoder'


def rms_norm(x, g):
    xf = x.astype(jnp.float32)
    y = xf * lax.rsqrt(jnp.mean(xf * xf, axis=-1, keepdims=True) + EPS)
    return (y * g.astype(jnp.float32)).astype(x.dtype)


def rope_tables(seq):
    inv_freq = ROPE_THETA ** (-jnp.arange(0, QK_ROPE_DIM, 2, dtype=jnp.float32) / QK_ROPE_DIM)
    ang = jnp.arange(seq, dtype=jnp.float32)[:, None] * inv_freq[None, :]
    return jnp.cos(ang), jnp.sin(ang)


def apply_rope(x, cos, sin):
    xf = x.astype(jnp.float32)
    x1, x2 = jnp.split(xf, 2, axis=-1)
    c = cos[:, None, :]
    s = sin[:, None, :]
    return jnp.concatenate([x1 * c - x2 * s, x2 * c + x1 * s], axis=-1).astype(x.dtype)


def pool_mixer(u, w_pool, pool_scale):
    B, S, _ = u.shape
    uf = u.astype(jnp.float32)
    cs = jnp.concatenate([jnp.zeros((B, 1, POOL_WIDTH), jnp.float32), jnp.cumsum(uf, axis=1)], axis=1)
    t = jnp.arange(S)
    outs = []
    for g, w in enumerate(POOL_WINDOWS):
        lo_c = g * POOL_GROUP_DIM
        hi_c = (g + 1) * POOL_GROUP_DIM
        lo = jnp.clip(t - w // 2, 0, S)
        hi = jnp.clip(t + w // 2, 0, S)
        csg = cs[:, :, lo_c:hi_c]
        window_sum = jnp.take(csg, hi, axis=1) - jnp.take(csg, lo, axis=1)
        count = (hi - lo).astype(jnp.float32)[None, :, None]
        outs.append(window_sum / count - uf[:, :, lo_c:hi_c])
    pooled = jnp.stack(outs, axis=2).astype(u.dtype)
    mixed = jnp.einsum('bsgc,gcd->bsgd', pooled, w_pool).reshape(B, S, POOL_WIDTH)
    return mixed * pool_scale


def mla(q_lat, kv_lat, k_rope_raw, q_norm_g, w_uq, kv_norm_g, w_ukv, cos, sin):
    B, S, _ = q_lat.shape
    q = jnp.einsum('bsr,rn->bsn', rms_norm(q_lat, q_norm_g), w_uq).reshape(B, S, N_HEADS, QK_HEAD_DIM)
    q_nope, q_rope = jnp.split(q, [QK_NOPE_DIM], axis=-1)
    q_rope = apply_rope(q_rope, cos, sin)
    kv = jnp.einsum('bsr,rn->bsn', rms_norm(kv_lat, kv_norm_g), w_ukv).reshape(B, S, N_HEADS, QK_NOPE_DIM + V_HEAD_DIM)
    k_nope, v = jnp.split(kv, [QK_NOPE_DIM], axis=-1)
    k_rope = apply_rope(k_rope_raw[:, :, None, :], cos, sin)
    q_full = jnp.concatenate([q_nope, q_rope], axis=-1)
    k_full = jnp.concatenate([k_nope, jnp.broadcast_to(k_rope, (B, S, N_HEADS, QK_ROPE_DIM))], axis=-1)
    scale = 1.0 / math.sqrt(QK_HEAD_DIM)
    n_blk = S // Q_BLOCK
    qb = q_full.reshape(B, n_blk, Q_BLOCK, N_HEADS, QK_HEAD_DIM).transpose(1, 0, 2, 3, 4)

    def attend(q_blk):
        s = jnp.einsum('bqhd,bkhd->bhqk', q_blk, k_full).astype(jnp.float32) * scale
        p = jax.nn.softmax(s, axis=-1).astype(v.dtype)
        return jnp.einsum('bhqk,bkhd->bqhd', p, v)

    o = lax.map(attend, qb)
    return o.transpose(1, 0, 2, 3, 4).reshape(B, S, ATTN_WIDTH)


def conv_ffn(h, w_up, conv_w, conv_b, w_down):
    S = h.shape[1]
    u = jnp.einsum('bsd,df->bsf', h, w_up)
    pad = CONV_WIDTH // 2
    up = jnp.pad(u, ((0, 0), (pad, pad), (0, 0)))
    c = conv_b
    for k in range(CONV_WIDTH):
        c = c + up[:, k:k + S] * conv_w[k]
    gate, val = jnp.split(c, 2, axis=-1)
    return jnp.einsum('bsf,fd->bsd', jax.nn.silu(gate) * val, w_down)


def encoder_forward(x, norm_mix_g, w_in, q_norm_g, w_uq, kv_norm_g, w_ukv, w_pool, pool_scale,
                    w_out, norm_ffn_g, w_up, conv_w, conv_b, w_down, final_norm_g):
    S = x.shape[1]
    cos, sin = rope_tables(S)
    for l in range(DEPTH):
        h = rms_norm(x, norm_mix_g[l])
        z = jnp.einsum('bsd,dn->bsn', h, w_in[l])
        u_pool, q_lat, kv_lat, k_rope_raw = jnp.split(
            z, [POOL_WIDTH, POOL_WIDTH + Q_LORA_RANK, POOL_WIDTH + Q_LORA_RANK + KV_LORA_RANK], axis=-1)
        y_pool = pool_mixer(u_pool, w_pool[l], pool_scale[l])
        y_attn = mla(q_lat, kv_lat, k_rope_raw, q_norm_g[l], w_uq[l], kv_norm_g[l], w_ukv[l], cos, sin)
        mixed = jnp.concatenate([y_pool, y_attn], axis=-1)
        x = x + jnp.einsum('bsm,md->bsd', mixed, w_out[l])
        h = rms_norm(x, norm_ffn_g[l])
        x = x + conv_ffn(h, w_up[l], conv_w[l], conv_b[l], w_down[l])
    return rms_norm(x, final_norm_g)


def setup_inputs(seed: int = 0) -> dict:
    key = jax.random.key(seed)
    ks = jax.random.split(key, 20)
    f32 = jnp.float32

    def normal(k, shape, scale):
        return jax.random.normal(k, shape, f32) * scale

    def gain(k, shape):
        return 1.0 + 0.02 * jax.random.normal(k, shape, f32)

    return {
        'x_prompt': normal(ks[0], (BATCH, SEQ, D_MODEL), 1.0),
        'x_sample': normal(ks[1], (DEC_BATCH, DEC_SEQ, D_MODEL), 1.0),
        'norm_mix_g': gain(ks[2], (DEPTH, D_MODEL)),
        'w_in': normal(ks[3], (DEPTH, D_MODEL, IN_PROJ_WIDTH), D_MODEL ** -0.5),
        'q_norm_g': gain(ks[4], (DEPTH, Q_LORA_RANK)),
        'w_uq': normal(ks[5], (DEPTH, Q_LORA_RANK, N_HEADS * QK_HEAD_DIM), Q_LORA_RANK ** -0.5),
        'kv_norm_g': gain(ks[6], (DEPTH, KV_LORA_RANK)),
        'w_ukv': normal(ks[7], (DEPTH, KV_LORA_RANK, N_HEADS * (QK_NOPE_DIM + V_HEAD_DIM)), KV_LORA_RANK ** -0.5),
        'w_pool': normal(ks[8], (DEPTH, POOL_GROUPS, POOL_GROUP_DIM, POOL_GROUP_DIM), POOL_GROUP_DIM ** -0.5),
        'pool_scale': gain(ks[9], (DEPTH, POOL_WIDTH)),
        'w_out': normal(ks[10], (DEPTH, MIX_WIDTH, D_MODEL), MIX_WIDTH ** -0.5),
        'norm_ffn_g': gain(ks[11], (DEPTH, D_MODEL)),
        'w_up': normal(ks[12], (DEPTH, D_MODEL, 2 * D_FF), D_MODEL ** -0.5),
        'conv_w': normal(ks[13], (DEPTH, CONV_WIDTH, 2 * D_FF), CONV_WIDTH ** -0.5),
        'conv_b': normal(ks[14], (DEPTH, 2 * D_FF), 0.02),
        'w_down': normal(ks[15], (DEPTH, D_FF, D_MODEL), D_FF ** -0.5),
        'final_norm_g': gain(ks[16], (D_MODEL,)),
    }


def reference(x_prompt, x_sample, norm_mix_g, w_in, q_norm_g, w_uq, kv_norm_g, w_ukv, w_pool, pool_scale,
              w_out, norm_ffn_g, w_up, conv_w, conv_b, w_down, final_norm_g):
    y_prompt = encoder_forward(x_prompt, norm_mix_g, w_in, q_norm_g, w_uq, kv_norm_g, w_ukv, w_pool, pool_scale,
                               w_out, norm_ffn_g, w_up, conv_w, conv_b, w_down, final_norm_g)
    y_sample = encoder_forward(x_sample, norm_mix_g, w_in, q_norm_g, w_uq, kv_norm_g, w_ukv, w_pool, pool_scale,
                               w_out, norm_ffn_g, w_up, conv_w, conv_b, w_down, final_norm_g)
    return (y_prompt, y_sample)
```

```python
import math
from contextlib import ExitStack

import numpy as np
import concourse.bass as bass
import concourse.mybir as mybir
from concourse.bass_utils import run_bass_kernel_spmd

F32 = mybir.dt.float32
BF16 = mybir.dt.bfloat16
AF = mybir.ActivationFunctionType
ALU = mybir.AluOpType

D = 1024
NSEG = 6
EXT = 1280
DFF = 2816
NJ = 22
SC = 1.0 / math.sqrt(96.0)
PIPE = True
ATT_PIPE = True
ACT_COPY = False
PREP_ACT = True
BG_POOL = True
PREP_EARLY = True
COMPUTE = ("act", "pool", "dve", "pe")
ENGS = ("sync", "act", "pool", "dve", "pe")


class Op:
    __slots__ = ("eng", "fn", "deps", "dma_key", "dma_cnt", "signal", "sig_idx", "blk")


class Sched:
    def __init__(self, nc, es):
        self.nc = nc
        self.es = es
        self.pending = []
        self.lastw = {}
        self.readers = {}
        self.sem = {e: es.enter_context(nc.semaphore("c_" + e)) for e in COMPUTE}
        self.count = {e: 0 for e in COMPUTE}
        self.dma_sem = {}
        self.dma_cnt = {}
        self.dma_last = {}
        self.waited = {e: {} for e in ENGS}
        self.blk = 0
        self.log = {e: [] for e in ENGS}
        self.bank_last = {}

    EXPAND = {"P0": ("P0a", "P0b"), "P1": ("P1a", "P1b"), "P2": ("P2a", "P2b"), "P3": ("P3a", "P3b"),
              "PO": ("PO", "P2a", "P2b"),
              "PY": ("P2a", "P2b", "P3a", "P3b"), "P2_0": ("P2a",), "P2_1": ("P2b",),
              "P3_0": ("P3a",), "P3_1": ("P3b",), "POH": ("POH", "P3b")}
    BANKS = frozenset(("P0a", "P0b", "P1a", "P1b", "P2a", "P2b", "P3a", "P3b"))

    def _expand(self, names):
        out = []
        for n in names:
            out.extend(self.EXPAND.get(n, (n,)))
        return out

    def op(self, eng, fn, reads=(), writes=(), dma=None):
        o = Op()
        o.eng, o.fn, o.dma_key, o.signal, o.sig_idx, o.blk = eng, fn, dma, False, 0, self.blk
        reads = self._expand(reads)
        writes = self._expand(writes)
        deps = set()
        for r in set(reads) | set(writes):
            if r in self.BANKS:
                bl = self.bank_last.setdefault(r, {})
                for e2, o2 in bl.items():
                    if e2 != eng:
                        deps.add(o2)
                bl[eng] = o
        for r in reads:
            w = self.lastw.get(r)
            if w is not None:
                deps.add(w)
        for r in writes:
            w = self.lastw.get(r)
            if w is not None:
                deps.add(w)
            deps.update(self.readers.get(r, ()))
        if dma is not None:
            p = self.dma_last.get(dma)
            if p is not None:
                deps.add(p)
            self.dma_last[dma] = o
            if dma not in self.dma_sem:
                self.dma_sem[dma] = self.es.enter_context(self.nc.semaphore("d_" + dma))
                self.dma_cnt[dma] = 0
            self.dma_cnt[dma] += 1
            o.dma_cnt = self.dma_cnt[dma]
        else:
            o.dma_cnt = 0
        deps.discard(o)
        o.deps = deps
        for r in reads:
            self.readers.setdefault(r, []).append(o)
        for r in writes:
            self.lastw[r] = o
            self.readers[r] = []
        self.pending.append(o)
        return o

    def flush(self, block):
        ops = self.pending
        self.pending = []
        blk = self.blk
        for o in ops:
            for d in o.deps:
                if d.blk != blk or d.dma_key is not None:
                    continue
                if d.eng == "pe" and o.eng == "pe":
                    continue
                d.signal = True
        last = {}
        for o in ops:
            if o.dma_key is None:
                last[o.eng] = o
        for o in last.values():
            o.signal = True
        start_counts = dict(self.count)
        start_dma = {k: v for k, v in self.dma_cnt.items()}
        for o in ops:
            if o.dma_key is not None:
                start_dma[o.dma_key] = min(start_dma[o.dma_key], o.dma_cnt - 1)
        for o in ops:
            if o.dma_key is None and o.signal:
                self.count[o.eng] += 1
                o.sig_idx = self.count[o.eng]

        def body_for(eng):
            def body(e):
                wd = self.waited[eng]

                lg = self.log[eng]

                def wait(key, sem, val):
                    if val > 0 and wd.get(key, 0) < val:
                        e.wait_ge(sem, val)
                        wd[key] = val
                        lg.append(("w", key, val))

                for ce in COMPUTE:
                    wait(("c", ce), self.sem[ce], start_counts[ce])
                for k, c in start_dma.items():
                    wait(("d", k), self.dma_sem[k], 16 * c)
                for o in ops:
                    if o.eng != eng:
                        continue
                    for d in o.deps:
                        if d.blk != blk:
                            continue
                        if d.dma_key is not None:
                            wait(("d", d.dma_key), self.dma_sem[d.dma_key], 16 * d.dma_cnt)
                        elif d.eng == "pe" and eng == "pe":
                            continue
                        else:
                            wait(("c", d.eng), self.sem[d.eng], d.sig_idx)
                    ins = o.fn(e)
                    if o.dma_key is not None:
                        ins.then_inc(self.dma_sem[o.dma_key], 16)
                        lg.append(("i", ("d", o.dma_key), 16))
                    elif o.signal:
                        ins.then_inc(self.sem[eng], 1)
                        lg.append(("i", ("c", eng), 1))
            return body

        block.sync(body_for("sync"))
        block.scalar(body_for("act"))
        block.gpsimd(body_for("pool"))
        block.vector(body_for("dve"))
        block.tensor(body_for("pe"))
        self.blk += 1

    def final_wait(self, block):
        def body(e):
            for ce in COMPUTE:
                if self.count[ce] > 0:
                    e.wait_ge(self.sem[ce], self.count[ce])
            for k, c in self.dma_cnt.items():
                e.wait_ge(self.dma_sem[k], 16 * c)
        block.sync(body)


def build_program():
    nc = bass.Bass("TRN2", target_bir_lowering=False)

    def din(name, shape):
        return nc.dram_tensor(name, list(shape), F32, kind="ExternalInput").ap()

    xext = din("xext", (NSEG, EXT, D))
    xoth_p = din("xoth_p", (2, 1024, D))
    xoth_s = din("xoth_s", (7168, D))
    ropeq = din("ropeq", (NSEG, 128, 10, 64))
    ropek_p = din("ropek_p", (2, 8, 128, 64))
    ropek_s = din("ropek_s", (56, 128, 64))
    masks_d = din("masks", (128, 12))
    band_c_d = din("band_c", (128, 12, 128))
    band_f_d = din("band_f", (NSEG, 128, 4, 128))
    band_l_d = din("band_l", (NSEG, 128, 4, 128))
    w_in_d = din("w_in_h", (128, 8, 928))
    g_mix_d = din("g_mix", (128, 8))
    w_uq_d = din("w_uq_h", (128, 2, 768))
    g_q_d = din("g_q", (128, 2))
    w_ukv_d = din("w_ukv_h", (128, 1, 1024))
    g_kv_d = din("g_kv", (128, 1))
    w_pool_d = din("w_pool_h", (128, 4, 128))
    w_out_d = din("w_out_h", (128, 8, 1024))
    pscale_d = din("pscale", (128, 4))
    w_up_d = din("w_up_h", (NJ, 128, 8, 256))
    g_ffn_d = din("g_ffn", (128, 8))
    w_down_d = din("w_down_h", (11, 128, 4, 512))
    convp_d = din("convp", (128, NJ * 8))
    gfin_d = din("gfin", (128, D))
    ident_d = din("ident", (128, 128))
    y_out = nc.dram_tensor("y", [NSEG, 1024, D], F32, kind="ExternalOutput").ap()

    def dscr(name, shape):
        return nc.dram_tensor(name, list(shape), BF16, kind="Internal").ap()

    s_w_in = dscr("s_w_in", (128, 8, 928))
    s_w_uq = dscr("s_w_uq", (128, 2, 768))
    s_w_ukv = dscr("s_w_ukv", (128, 1, 1024))
    s_w_pool = dscr("s_w_pool", (128, 4, 128))
    s_w_out = dscr("s_w_out", (128, 8, 1024))
    s_w_up = dscr("s_w_up", (NJ, 128, 8, 256))
    s_w_down = dscr("s_w_down", (11, 128, 4, 512))
    s_band_c = dscr("s_band_c", (128, 12, 128))
    recd = nc.dram_tensor("recd", [3, 512], F32, kind="Internal").ap()

    with ExitStack() as es:
        E = es.enter_context

        uid = [0]

        def sb(name, shape, dt, st=None):
            uid[0] += 1
            return (st or es).enter_context(nc.sbuf_tensor("sb%d_%s" % (uid[0], name), list(shape), dt))

        S = Sched(nc, es)
        P = [E(nc.psum_tensor("P%d" % i, [128, 1024], F32)) for i in range(4)]

        ident_f = sb("ident_f", (128, 128), F32)
        ident = sb("ident_b", (128, 128), BF16)
        epsb = sb("epsb", (128, 1), F32)
        ones_f = sb("ones_f", (128, 128), F32)
        w_out_b = sb("w_out_b", (128, 8, 1024), BF16)
        mixP = sb("mixP", (128, 4, 1026), BF16)
        attnT = sb("attnT", (128, 4, 1026), BF16)
        kv_nT = sb("kv_nT", (128, 8192), BF16)
        kT = sb("kT", (96, 8192), BF16)
        gfin = sb("gfin", (128, D), F32)
        convp = sb("convp", (128, NJ * 8), F32)
        masks = sb("masks", (128, 12), F32)
        gsm = sb("gsm", (128, 32), F32)
        stat = sb("stat", (128, 64), F32)

        def dma(key, out, in_, reads, writes):
            return S.op("sync", lambda e: e.dma_start(out=out, in_=in_), reads, writes, dma=key)

        def act(out, in_, func, reads, writes, scale=1.0, bias=None, accum=None):
            def fn(e):
                kw = {}
                if bias is not None:
                    kw["bias"] = bias
                if accum is not None:
                    kw["accum_out"] = accum
                return e.activation(out=out, in_=in_, func=func, scale=scale, **kw)
            return S.op("act", fn, reads, writes)

        def ts(eng, out, in0, s1, s2, op0, op1, reads, writes):
            if s2 is None:
                return S.op(eng, lambda e: e.tensor_scalar(out=out, in0=in0, scalar1=s1, scalar2=None, op0=op0), reads, writes)
            return S.op(eng, lambda e: e.tensor_scalar(out=out, in0=in0, scalar1=s1, scalar2=s2, op0=op0, op1=op1), reads, writes)

        def tt(eng, out, in0, in1, op, reads, writes):
            return S.op(eng, lambda e: e.tensor_tensor(out=out, in0=in0, in1=in1, op=op), reads, writes)

        def stt(eng, out, in0, scalar, in1, op0, op1, reads, writes):
            return S.op(eng, lambda e: e.scalar_tensor_tensor(out=out, in0=in0, scalar=scalar, in1=in1, op0=op0, op1=op1), reads, writes)

        def cp(eng, out, in_, reads, writes):
            return S.op(eng, lambda e: e.tensor_copy(out=out, in_=in_), reads, writes)

        def acp(out, in_, reads, writes):
            if ACT_COPY:
                return act(out, in_, AF.Identity, reads, writes)
            return cp("dve", out, in_, reads, writes)

        def mm(lst, reads, writes):
            def fn(e):
                ins = None
                for (o, l, r, st, sp) in lst:
                    ins = e.matmul(o, lhsT=l, rhs=r, start=st, stop=sp)
                return ins
            return S.op("pe", fn, reads, writes)

        def tr(lst, reads, writes):
            def fn(e):
                ins = None
                for (o, i, idn) in lst:
                    ins = e.transpose(out=o, in_=i, identity=idn)
                return ins
            return S.op("pe", fn, reads, writes)

        def rms_stats(src, n, pp, col, rsrc, tag, junk):
            ms = stat[0:pp, col:col + 1]
            ln = stat[0:pp, col + 1:col + 2]
            rs = stat[0:pp, col + 2:col + 3]
            act(junk, src, AF.Square, [rsrc], [tag + "ms"], scale=float(n) ** -0.5, accum=ms)
            act(ln, ms, AF.Ln, [tag + "ms", "epsb"], [tag + "ln"], bias=epsb[0:pp, :])
            act(rs, ln, AF.Exp, [tag + "ln"], [tag + "rs"], scale=-0.5)
            return rs

        with ExitStack() as st0:
            stg32 = [sb("stg32_%d" % i, (128, 2048), F32, st0) for i in range(2)]
            stg16 = [sb("stg16_%d" % i, (128, 2048), BF16, st0) for i in range(2)]
            S.op("dve", lambda e: e.memset(epsb[:], 1e-6), [], ["epsb"])
            S.op("dve", lambda e: e.memset(ones_f[:], 1.0), [], ["ones_f"])
            dma("c0", ident_f[:], ident_d, [], ["ident_f"])
            cp("dve", ident[:], ident_f[:], ["ident_f"], ["ident"])
            dma("c1", gfin[:], gfin_d, [], ["gfin"])
            dma("c2", convp[:], convp_d, [], ["convp"])
            dma("c3", masks[:], masks_d, [], ["masks"])
            dma("c4", gsm[:, 0:8], g_mix_d, [], ["gsm"])
            dma("c5", gsm[:, 8:10], g_q_d, [], ["gsm"])
            dma("c6", gsm[:, 10:11], g_kv_d, [], ["gsm"])
            dma("c7", gsm[:, 11:15], pscale_d, [], ["gsm"])
            dma("c8", gsm[:, 16:24], g_ffn_d, [], ["gsm"])
            cnt = [0]

            def conv(src, dst, a, bdim, scale_ap):
                b = cnt[0] % 2
                cnt[0] += 1
                n = a * bdim
                v32 = stg32[b][:, 0:n].rearrange("p (a b) -> p a b", a=a)
                v16 = stg16[b][:, 0:n].rearrange("p (a b) -> p a b", a=a)
                dma("ld32_%d" % b, v32, src, [], ["stg32_%d" % b])
                if scale_ap is None:
                    cp("dve", v16, v32, ["stg32_%d" % b], ["stg16_%d" % b])
                else:
                    tt("dve", v16, v32, scale_ap.unsqueeze(2).to_broadcast([128, a, bdim]), ALU.mult,
                       ["stg32_%d" % b, "gsm"], ["stg16_%d" % b])
                dma("st16_%d" % b, dst, v16, ["stg16_%d" % b], ["scr"])

            for c in range(4):
                conv(w_in_d[:, 2 * c:2 * c + 2, :], s_w_in[:, 2 * c:2 * c + 2, :], 2, 928, gsm[:, 2 * c:2 * c + 2])
            conv(w_uq_d, s_w_uq, 2, 768, gsm[:, 8:10])
            conv(w_ukv_d, s_w_ukv, 1, 1024, gsm[:, 10:11])
            conv(w_pool_d, s_w_pool, 4, 128, None)
            for c in range(4):
                conv(w_out_d[:, 2 * c:2 * c + 2, :], s_w_out[:, 2 * c:2 * c + 2, :], 2, 1024,
                     gsm[:, 11 + 2 * c:13 + 2 * c] if c < 2 else None)
            conv(band_c_d, s_band_c, 12, 128, None)
            dma("c9", w_out_b[:], s_w_out, ["scr"], ["w_out_b"])
            with nc.Block() as blk:
                S.flush(blk)

        groups = [("p", 0, 2048), ("p", 1, 2048), ("s", 0, 8192)]
        for gi, (gkind, gidx, TK) in enumerate(groups):
            NKT = TK // 128
            for half in range(2):
                seg = gi * 2 + half
                with ExitStack() as sa:
                    w_in_b = sb("w_in_b", (128, 8, 928), BF16, sa)
                    w_uq_b = sb("w_uq_b", (128, 2, 768), BF16, sa)
                    w_ukv_b = sb("w_ukv_b", (128, 1024), BF16, sa)
                    w_pool_b = sb("w_pool_b", (128, 4, 128), BF16, sa)
                    band_c = sb("band_c", (128, 12, 128), BF16, sa)
                    band_fl32 = sb("band_fl32", (128, 8, 128), F32, sa)
                    band_fl = sb("band_fl", (128, 8, 128), BF16, sa)
                    rq = sb("rq", (128, 10, 64), F32, sa)
                    rk = [sb("rk%d" % i, (128, 64), F32, sa) for i in range(4)]
                    upool = sb("upool", (128, 10, 512), BF16, sa)
                    q_nT = sb("q_nT", (128, 2, EXT), BF16, sa)
                    Vx = sb("Vx", (128, 64, 128), BF16, sa)
                    qT = sb("qT", (96, 8, 1026), BF16, sa)
                    NX = 2 if seg == 0 else 3
                    xt = [sb("xt%d" % i, (128, D), F32, sa) for i in range(NX)]
                    xn = [sb("xn%d" % i, (128, D), BF16, sa) for i in range(2)]
                    hT = [sb("hT%d" % i, (128, 8, 128), BF16, sa) for i in range(2)]
                    junk = sb("junkA", (128, D), BF16, sa)
                    stage = [sb("stage%d" % i, (128, 480), BF16, sa) for i in range(2)]
                    rtmp = [sb("rtmp%d" % i, (128, 8, 96), F32, sa) for i in range(2)]
                    qtm = [sb("qtm%d" % i, (128, 768), BF16, sa) for i in range(2)]
                    pooled = [sb("pooled%d" % i, (128, 4, 128), BF16, sa) for i in range(2)]
                    PTs = [sb("PTs%d" % i, (128, 1024), BF16, sa) for i in range(3)]
                    PHs = [sb("PHs%d" % i, (128, 32), BF16, sa) for i in range(2)]
                    otmp = [sb("otmp%d" % i, (128, 512), F32, sa) for i in range(2)]
                    rec = [sb("rec%d" % i, (128, 512), F32, sa) for i in range(2)]
                    ohacc = sb("ohacc", (128, 2), F32, sa)
                    oth = sb("oth", (128, 2), F32, sa)
                    rech = sb("rech", (128, 2), F32, sa)

                    dma("x0", xt[0][:], xext[seg, 0:128, :], [], ["xt0"])
                    dma("a0", w_in_b[:], s_w_in, ["scr"], ["w_in_b"])
                    dma("a7", rq[:], ropeq[seg], [], ["rq"])
                    dma("a1", w_uq_b[:], s_w_uq, ["scr"], ["w_uq_b"])
                    dma("a2", w_ukv_b[:], s_w_ukv[:, 0, :], ["scr"], ["w_ukv_b"])
                    dma("a3", w_pool_b[:], s_w_pool, ["scr"], ["w_pool_b"])
                    dma("a4", band_c[:], s_band_c, ["scr"], ["band_c"])
                    dma("a5", band_fl32[:, 0:4, :], band_f_d[seg], [], ["band_fl32"])
                    dma("a6", band_fl32[:, 4:8, :], band_l_d[seg], [], ["band_fl32"])
                    cp("dve", band_fl[:], band_fl32[:], ["band_fl32"], ["band_fl"])
                    for i in range(2):
                        S.op("pool", lambda e, t=stage[i]: e.memset(t[:, 384:448], 0.0), [], ["stagez%d" % i])

                    def p1_stages(t, src_ap, full, kcol, rope_ap, rope_res, ext_i, pre=None):
                        b = t % 2
                        xb_ = t % NX
                        X, XN, HT, ST = "xt%d" % xb_, "xn%d" % b, "hT%d" % b, "stage%d" % b
                        PTb = P[3][:, b * 512:(b + 1) * 512].bitcast(BF16).rearrange("p (a b) -> p a b", a=8)
                        PSb = P[2][:, b * 512:(b + 1) * 512].bitcast(BF16).rearrange("p (a b) -> p a b", a=8)
                        PZ = P[b]
                        PTr, PSr, PZr = "P3_%d" % b, "P2_%d" % b, "P%d" % b

                        def stA():
                            if pre is not None:
                                pre()
                            if t != 0:
                                dma("x%d" % xb_, xt[xb_][:], src_ap, [], [X])
                            rs = rms_stats(xt[xb_][:], D, 128, 0 + 16 * b, X, "n1_%d" % b, junk[:])
                            ts("dve", xn[b][:], xt[xb_][:], rs, None, ALU.mult, None, [X, "n1_%drs" % b], [XN])

                        def stB():
                            tr([(PTb[:, c, :], xn[b][:, c * 128:(c + 1) * 128], ident[:]) for c in range(8)],
                               [XN, "ident"], [PTr])
                            cp("dve", hT[b][:], PTb, [PTr], [HT])

                        def stC():
                            lst = []
                            if full:
                                for c in range(8):
                                    lst.append((PZ[:, 0:512], hT[b][:, c, :], w_in_b[:, c, 0:512], c == 0, c == 7))
                            z0 = 512 if full else 768
                            for c in range(8):
                                lst.append((PZ[:, z0:928], hT[b][:, c, :], w_in_b[:, c, z0:928], c == 0, c == 7))
                            mm(lst, [HT, "w_in_b"], [PZr])
                            Ba, Bb = "P%da" % b, "P%db" % b
                            rq_ = rk_ = None
                            if full:
                                rq_ = rms_stats(PZ[:, 512:768], 256, 128, 4 + 16 * b, Bb, "nq_%d" % b, junk[:, 0:256])
                            if kcol is not None:
                                rk_ = rms_stats(PZ[:, 768:896], 128, 128, 8 + 16 * b, Bb, "nk_%d" % b, junk[:, 0:128])
                            if full:
                                act(upool[:, ext_i, :], PZ[:, 0:512], AF.Identity, [Ba], ["upool%d" % ext_i])
                                ts("dve", stage[b][:, 0:256], PZ[:, 512:768], rq_, None, ALU.mult, None,
                                   [Bb, "nq_%drs" % b], [ST + "q"])
                            if kcol is not None:
                                ts("dve", stage[b][:, 256:384], PZ[:, 768:896], rk_, None, ALU.mult, None,
                                   [Bb, "nk_%drs" % b], [ST + "k"])
                                rt = rtmp[b]
                                RT = "rtmp%d" % b
                                tt("dve", rt[:, 0, 0:32], PZ[:, 896:928], rope_ap[:, 0:32], ALU.mult, [Bb, rope_res], [RT + "a"])
                                tt("dve", rt[:, 0, 32:48], PZ[:, 912:928], rope_ap[:, 32:48], ALU.mult, [Bb, rope_res], [RT + "b"])
                                tt("dve", rt[:, 0, 48:64], PZ[:, 896:912], rope_ap[:, 48:64], ALU.mult, [Bb, rope_res], [RT + "c"])
                                tt("dve", stage[b][:, 448:480], rt[:, 0, 0:32], rt[:, 0, 32:64], ALU.add,
                                   [RT + "a", RT + "b", RT + "c"], [ST + "kr"])

                        def stD():
                            lst = []
                            rd = ["ident"]
                            if full:
                                lst += [(PSb[:, 0, :], stage[b][:, 0:128], ident[:]), (PSb[:, 1, :], stage[b][:, 128:256], ident[:])]
                                rd.append(ST + "q")
                            if kcol is not None:
                                lst += [(PSb[:, 2, :], stage[b][:, 256:384], ident[:]), (PSb[0:96, 3, :], stage[b][:, 384:480], ident[:])]
                                rd += [ST + "k", ST + "kr", "stagez%d" % b]
                            if lst:
                                tr(lst, rd, [PSr])
                            if full:
                                cp("dve", q_nT[:, :, ext_i * 128:(ext_i + 1) * 128], PSb[:, 0:2, :], [PSr], ["q_nT%d" % ext_i])
                            if kcol is not None:
                                cp("dve", kv_nT[:, kcol:kcol + 128], PSb[:, 2, :], [PSr], ["kv_nT"])
                                cp("dve", kT[64:96, kcol:kcol + 128], PSb[64:96, 3, :], [PSr], ["kTr"])

                        return (stA, stB, stC, stD)

                    def run_pipe(stage_lists):
                        n = len(stage_lists)
                        depth = len(stage_lists[0])
                        if not PIPE:
                            for sl in stage_lists:
                                for f_ in sl:
                                    f_()
                            return
                        for s_ in range(n + depth - 1):
                            for k in range(depth):
                                idx = s_ - k
                                if 0 <= idx < n:
                                    stage_lists[idx][k]()

                    tl = []
                    for i in range(10):
                        kcol = (i - 1) * 128 if (half == 0 and 1 <= i <= 8) else None
                        tl.append(p1_stages(len(tl), xext[seg, i * 128:(i + 1) * 128, :], True, kcol, rq[:, i, :], "rq", i))
                    if half == 0:
                        noth = NKT - 8
                        for j in range(noth):
                            b2 = j % 4
                            if gkind == "p":
                                srcx = xoth_p[gidx, j * 128:(j + 1) * 128, :]
                                srcr = ropek_p[gidx, j]
                            else:
                                srcx = xoth_s[j * 128:(j + 1) * 128, :]
                                srcr = ropek_s[j]

                            def pre(b2=b2, srcr=srcr):
                                dma("rk%d" % b2, rk[b2][:], srcr, [], ["rk%d" % b2])
                            tl.append(p1_stages(len(tl), srcx, False, 1024 + j * 128, rk[b2][:], "rk%d" % b2, None, pre))
                    run_pipe(tl)

                    def q_stages(i):
                        b = i % 2
                        PZ = P[b]
                        PZr = "P%d" % b
                        PTb = P[3][:, b * 512:(b + 1) * 512].bitcast(BF16).rearrange("p (a b) -> p a b", a=8)
                        PTr = "P3_%d" % b
                        cols = slice(i * 128, (i + 1) * 128)
                        QT = "qtm%d" % b

                        def q1():
                            lst = []
                            for c in range(2):
                                lst.append((PZ[:, 0:512], q_nT[:, c, cols], w_uq_b[:, c, 0:512], c == 0, c == 1))
                            for c in range(2):
                                lst.append((PZ[:, 512:768], q_nT[:, c, cols], w_uq_b[:, c, 512:768], c == 0, c == 1))
                            mm(lst, ["q_nT%d" % i, "w_uq_b"], [PZr])
                            q3 = PZ[:, 0:768].rearrange("p (h d) -> p h d", h=8)
                            o3 = qtm[b][:].rearrange("p (h d) -> p h d", h=8)
                            rt = rtmp[b]
                            RT = "rtmp%d" % b
                            acp(o3[:, :, 0:64], q3[:, :, 0:64], [PZr], [QT + "n"])
                            C2 = rq[:, i, 0:32].unsqueeze(1).to_broadcast([128, 8, 32])
                            Sa = rq[:, i, 32:48].unsqueeze(1).to_broadcast([128, 8, 16])
                            Sb = rq[:, i, 48:64].unsqueeze(1).to_broadcast([128, 8, 16])
                            tt("dve", rt[:, :, 0:32], q3[:, :, 64:96], C2, ALU.mult, [PZr, "rq"], [RT + "a"])
                            tt("dve", rt[:, :, 32:48], q3[:, :, 80:96], Sa, ALU.mult, [PZr, "rq"], [RT + "b"])
                            tt("dve", rt[:, :, 48:64], q3[:, :, 64:80], Sb, ALU.mult, [PZr, "rq"], [RT + "c"])
                            tt("dve", o3[:, :, 64:96], rt[:, :, 0:32], rt[:, :, 32:64], ALU.add,
                               [RT + "a", RT + "b", RT + "c"], [QT + "r"])

                        def q2():
                            tr([(PTb[0:96, h, :], qtm[b][:, h * 96:(h + 1) * 96], ident[:]) for h in range(8)],
                               [QT + "n", QT + "r", "ident"], [PTr])
                            if i == 0:
                                cp("dve", qT[:, :, 0:1], PTb[0:96, :, 127:128], [PTr], ["qT"])
                            elif i == 9:
                                cp("dve", qT[:, :, 1025:1026], PTb[0:96, :, 0:1], [PTr], ["qT"])
                            else:
                                cp("dve", qT[:, :, 1 + (i - 1) * 128:1 + i * 128], PTb[0:96, :, :], [PTr], ["qT"])

                        return (q1, q2)

                    run_pipe([q_stages(i) for i in range(10)])

                    def pool_stages(items, ncols, qc, b):
                        PP = P[2][:, 0:512].rearrange("p (g t) -> p g t", g=4)
                        PM = P[2][:, 512:1024].rearrange("p (g t) -> p g t", g=4)

                        def pp1():
                            lst = []
                            rd = ["band_c", "band_fl"]
                            for g in range(4):
                                its = items[g]
                                for n_, (kt_, rhs) in enumerate(its):
                                    lst.append((PP[:, g, 0:ncols], upool[:, kt_, g * 128:(g + 1) * 128], rhs, n_ == 0, n_ == len(its) - 1))
                                    rd.append("upool%d" % kt_)
                            mm(lst, rd, ["P2_0"])
                            cp("dve", pooled[b][:, :, 0:ncols], PP[:, :, 0:ncols], ["P2_0"], ["pooled%d" % b])

                        def pp2():
                            mm([(PM[:, g, 0:ncols], w_pool_b[:, g, :], pooled[b][:, g, 0:ncols], True, True) for g in range(4)],
                               ["pooled%d" % b, "w_pool_b"], ["P2_1"])
                            acp(mixP[:, :, qc:qc + ncols], PM[:, :, 0:ncols], ["P2_1"], ["mixP"])

                        return (pp1, pp2)

                    pl = []
                    for i in range(1, 9):
                        items = []
                        for g in range(4):
                            if i == 1:
                                cur = band_fl[:, g, :]
                            elif i == 8:
                                cur = band_fl[:, 4 + g, :]
                            else:
                                cur = band_c[:, 4 + g, :]
                            items.append([(i - 1, band_c[:, g, :]), (i, cur), (i + 1, band_c[:, 8 + g, :])])
                        pl.append(pool_stages(items, 128, 1 + (i - 1) * 128, i % 2))
                    pl.append(pool_stages([[(0, band_c[:, 4 + g, 127:128]), (1, band_c[:, 8 + g, 127:128])] for g in range(4)], 1, 0, 1))
                    pl.append(pool_stages([[(8, band_c[:, g, 0:1]), (9, band_c[:, 4 + g, 0:1])] for g in range(4)], 1, 1025, 0))
                    run_pipe(pl)

                    bg_work = []
                    if seg == 0:
                        sg32 = [sb("sgA32_%d" % i, (128, 1024), F32, sa) for i in range(2)]
                        sg16 = [sb("sgA16_%d" % i, (128, 1024), BF16, sa) for i in range(2)]
                        bgc = [0]

                        chunks = []
                        for j in range(NJ):
                            for hh_ in range(2):
                                chunks.append((w_up_d[j][:, 4 * hh_:4 * hh_ + 4, :], s_w_up[j][:, 4 * hh_:4 * hh_ + 4, :],
                                               4, 256, gsm[:, 16 + 4 * hh_:20 + 4 * hh_]))
                        for j in range(11):
                            for hh_ in range(2):
                                chunks.append((w_down_d[j][:, 2 * hh_:2 * hh_ + 2, :], s_w_down[j][:, 2 * hh_:2 * hh_ + 2, :],
                                               2, 512, None))

                        def bg_load(i):
                            src, dst, a_, bdim, sc_ = chunks[i]
                            b_ = i % 2
                            v32 = sg32[b_][:, :].rearrange("p (a b) -> p a b", a=a_)
                            dma("bl32_%d" % b_, v32, src, [], ["sgA32_%d" % b_])

                        def bg_step(i):
                            def f_():
                                if i + 1 < len(chunks):
                                    bg_load(i + 1)
                                src, dst, a_, bdim, sc_ = chunks[i]
                                b_ = i % 2
                                v32 = sg32[b_][:, :].rearrange("p (a b) -> p a b", a=a_)
                                v16 = sg16[b_][:, :].rearrange("p (a b) -> p a b", a=a_)
                                if sc_ is None:
                                    cp("pool" if BG_POOL else "dve", v16, v32, ["sgA32_%d" % b_], ["sgA16_%d" % b_])
                                else:
                                    tt("pool" if BG_POOL else "dve", v16, v32, sc_.unsqueeze(2).to_broadcast([128, a_, bdim]), ALU.mult,
                                       ["sgA32_%d" % b_, "gsm"], ["sgA16_%d" % b_])
                                dma("bs16_%d" % b_, dst, v16, ["sgA16_%d" % b_], ["scr"])
                            return f_

                        bg_load(0)
                        bg_work = [bg_step(i) for i in range(len(chunks))]

                    P3a = P[3][:, 0:512]
                    PH = P[3][:, 512:576]
                    PO = P[2]
                    def head_prep(h):
                        odd = h % 2
                        o0 = 64 if odd else 0
                        meng = "dve" if seg == 0 else "pool"
                        if odd:
                            S.op(meng, lambda e, Vx=Vx: e.memset(Vx[:, 0:NKT, 0:64], 0.0), [], ["Vx"])
                            S.op(meng, lambda e, Vx=Vx: e.memset(Vx[:, 0:NKT, 0:1], 1.0), ["Vx"], ["Vx1"])
                        else:
                            S.op(meng, lambda e, Vx=Vx: e.memset(Vx[:, 0:NKT, 64:65], 1.0), ["Vx"], ["Vx1"])
                        rr = 0
                        for r in range(TK // 1024):
                            pb = P[rr % 2]
                            mm([(pb[0:64, 0:512], w_ukv_b[:, h * 128:h * 128 + 64], kv_nT[:, r * 1024:r * 1024 + 512], True, True),
                                (pb[0:64, 512:1024], w_ukv_b[:, h * 128:h * 128 + 64], kv_nT[:, r * 1024 + 512:r * 1024 + 1024], True, True)],
                               ["w_ukv_b", "kv_nT"], ["P%d" % (rr % 2)])
                            if rr % 2 == 1 or not PREP_ACT:
                                cp("dve", kT[0:64, r * 1024:(r + 1) * 1024], pb[0:64, :], ["P%d" % (rr % 2)], ["kTn"])
                            else:
                                act(kT[0:64, r * 1024:(r + 1) * 1024], pb[0:64, :], AF.Identity, ["P%d" % (rr % 2)], ["kTn"])
                            rr += 1
                        for r in range(NKT // 16):
                            pb = P[rr % 2]
                            mm([(pb[:, t16 * 64:(t16 + 1) * 64], kv_nT[:, (r * 16 + t16) * 128:(r * 16 + t16 + 1) * 128],
                                 w_ukv_b[:, h * 128 + 64:h * 128 + 128], True, True) for t16 in range(16)],
                               ["w_ukv_b", "kv_nT"], ["P%d" % (rr % 2)])
                            src3 = pb[:, :].rearrange("p (a b) -> p a b", a=16)
                            if rr % 2 == 1 or not PREP_ACT:
                                cp("dve", Vx[:, r * 16:(r + 1) * 16, o0:o0 + 64], src3, ["P%d" % (rr % 2), "Vx1"], ["Vx"])
                            else:
                                act(Vx[:, r * 16:(r + 1) * 16, o0:o0 + 64], src3, AF.Identity, ["P%d" % (rr % 2), "Vx1"], ["Vx"])
                            rr += 1

                    norm_sched = {}
                    if PREP_EARLY:
                        head_prep(0)
                    for h in range(8):
                        if not PREP_EARLY:
                            head_prep(h)
                        odd = h % 2
                        MO = 128 if odd else 65
                        o0 = 64 if odd else 0
                        sp = 0 if odd else 64
                        POH = P[3][0:MO, 640:642]

                        def s_op(kt):
                            b = kt % 2
                            hb = (kt // 16) % 2
                            j16 = kt % 16
                            kTt = kT[0:96, kt * 128:(kt + 1) * 128]
                            mm([(P[b][:, 0:512], kTt, qT[0:96, h, 1:513], True, True),
                                (P[b][:, 512:1024], kTt, qT[0:96, h, 513:1025], True, True),
                                (P[3][:, 512 + hb * 32 + j16 * 2:512 + hb * 32 + j16 * 2 + 2], kTt, qT[0:96, h, 0:1026:1025], True, True)],
                               ["kTn", "kTr", "qT"], ["P%d" % b, "PH%d_%d" % (hb, j16), "P3_1"])

                        def pv_op(kt):
                            p3 = kt % 3
                            hb = (kt // 16) % 2
                            mm([(PO[0:MO, 0:512], Vx[:, kt, 0:MO], PTs[p3][:, 0:512], kt == 0, kt == NKT - 1),
                                (PO[0:MO, 512:1024], Vx[:, kt, 0:MO], PTs[p3][:, 512:1024], kt == 0, kt == NKT - 1)],
                               ["Vx", "Vx1", "PTs%d" % p3], ["PO", "P2_0", "P2_1"])
                            if kt % 16 == 15:
                                act(PHs[hb][:], P[3][:, 512 + hb * 32:512 + hb * 32 + 32], AF.Exp,
                                    ["PH%d_%d" % (hb, jj) for jj in range(16)] + ["P3_1"], ["PHs%d" % hb], scale=SC)
                                k0 = kt - 15
                                mm([(POH, Vx[:, k0 + jj, 0:MO], PHs[hb][:, jj * 2:jj * 2 + 2], jj == 0, jj == 15)
                                    for jj in range(16)], ["Vx", "Vx1", "PHs%d" % hb], ["POH", "P3_1"])
                                if k0 == 0:
                                    cp("dve", ohacc[0:MO, :], POH, ["POH"], ["ohacc"])
                                else:
                                    tt("dve", ohacc[0:MO, :], POH, ohacc[0:MO, :], ALU.add, ["POH", "ohacc"], ["ohacc"])

                        s_op(0)
                        for kt in range(NKT):
                            b = kt % 2
                            if kt + 1 < NKT:
                                s_op(kt + 1)
                            act(PTs[kt % 3][:], P[b][:], AF.Exp, ["P%d" % b], ["PTs%d" % (kt % 3)], scale=SC)
                            if kt >= 1:
                                pv_op(kt - 1)
                            if kt in norm_sched:
                                norm_sched.pop(kt)()
                            if bg_work:
                                bg_work.pop(0)()
                        pv_op(NKT - 1)
                        if PREP_EARLY and h + 1 < 8:
                            head_prep(h + 1)
                        for c in range(2):
                            cp("dve", otmp[c][0:MO, :], PO[0:MO, c * 512:(c + 1) * 512], ["PO"], ["otmp%d" % c])
                        cp("dve", oth[0:MO, :], ohacc[0:MO, :], ["ohacc"], ["oth"])

                        def mk_norm(h=h, MO=MO, o0=o0, sp=sp):
                            def n1():
                                for c in range(2):
                                    S.op("dve", lambda e, o=rec[c][sp:sp + 1, :], i_=otmp[c][sp:sp + 1, :]: e.reciprocal(out=o, in_=i_),
                                         ["otmp%d" % c], ["rec%d" % c, "recb%d" % c])
                                S.op("dve", lambda e, o=rech[sp:sp + 1, :], i_=oth[sp:sp + 1, :]: e.reciprocal(out=o, in_=i_),
                                     ["oth"], ["rech", "rechb"])

                            def n2():
                                for c in range(2):
                                    dma("nw%d" % c, recd[c:c + 1, :], rec[c][sp:sp + 1, :], ["rec%d" % c], ["recd%d" % c])
                                dma("nwh", recd[2:3, 0:2], rech[sp:sp + 1, :], ["rech"], ["recd2"])
                                for c in range(2):
                                    dma("nb%d" % c, rec[c][o0:o0 + 64, :], recd[c:c + 1, :].to_broadcast([64, 512]),
                                        ["recd%d" % c], ["recb%d" % c])
                                dma("nbh", rech[o0:o0 + 64, :], recd[2:3, 0:2].to_broadcast([64, 2]), ["recd2"], ["rechb"])

                            def n3(c):
                                def f_():
                                    tt("dve", attnT[o0:o0 + 64, h // 2, 1 + c * 512:1 + (c + 1) * 512], otmp[c][o0:o0 + 64, :],
                                       rec[c][o0:o0 + 64, :], ALU.mult, ["otmp%d" % c, "recb%d" % c], ["attnT"])
                                return f_

                            def n4():
                                tt("dve", attnT[o0:o0 + 64, h // 2, 0:1026:1025], oth[o0:o0 + 64, :], rech[o0:o0 + 64, :], ALU.mult,
                                   ["oth", "rechb"], ["attnT"])
                            return {0: n1, 4: n2, 9: n3(0), 11: n3(1), 13: n4}

                        norm_sched = mk_norm()
                    for k_ in sorted(norm_sched):
                        norm_sched[k_]()
                    while bg_work:
                        bg_work.pop(0)()
                    with nc.Block() as blk:
                        S.flush(blk)

                with ExitStack() as sbk:
                    xm = sb("xm", (128, 10, D), F32, sbk)
                    h2T = sb("h2T", (128, 8, 1026), BF16, sbk)
                    actT = sb("actT", (128, NJ, 512), BF16, sbk)
                    wu = [sb("wu%d" % i, (128, 8, 256), BF16, sbk) for i in range(2)]
                    wd = [sb("wd%d" % i, (128, 4, 512), BF16, sbk) for i in range(2)]
                    ga = [sb("ga%d" % i, (128, 512), F32, sbk) for i in range(2)]
                    va = [sb("va%d" % i, (128, 512), F32, sbk) for i in range(2)]
                    sg = [sb("sg%d" % i, (128, 512), F32, sbk) for i in range(2)]
                    tv = [sb("tv%d" % i, (128, 512), F32, sbk) for i in range(2)]
                    xtb = [sb("xtb%d" % i, (128, D), F32, sbk) for i in range(2)]
                    xn2 = [sb("xn2_%d" % i, (128, D), BF16, sbk) for i in range(2)]
                    junkb = sb("junkB", (128, D), BF16, sbk)
                    yo = [sb("yo%d" % i, (128, D), F32, sbk) for i in range(2)]

                    tiles = [(127, 1, 0, 0)] + [(128 * m, 128, 1 + (m - 1) * 128, m) for m in range(1, 9)] + [(1152, 1, 1025, 9)]

                    def xm_stages(ti, row, pp, qc, xi):
                        b = ti % 2
                        PZ = P[b]
                        PZr = "P%d" % b
                        PTb = P[3][:, b * 512:(b + 1) * 512].bitcast(BF16).rearrange("p (a b) -> p a b", a=8)
                        PTr = "P3_%d" % b
                        XB, XN2 = "xtb%d" % b, "xn2_%d" % b

                        def x1():
                            dma("xb%d" % b, xtb[b][0:pp, :], xext[seg, row:row + pp, :], [], [XB])
                            for hf in range(2):
                                lst = []
                                for g in range(4):
                                    lst.append((PZ[0:pp, hf * 512:(hf + 1) * 512], mixP[:, g, qc:qc + pp],
                                                w_out_b[:, g, hf * 512:(hf + 1) * 512], g == 0, False))
                                for hp in range(4):
                                    lst.append((PZ[0:pp, hf * 512:(hf + 1) * 512], attnT[:, hp, qc:qc + pp],
                                                w_out_b[:, 4 + hp, hf * 512:(hf + 1) * 512], False, hp == 3))
                                mm(lst, ["mixP", "attnT", "w_out_b"], [PZr])
                            tt("dve", xm[0:pp, xi, :], PZ[0:pp, :], xtb[b][0:pp, :], ALU.add, [PZr, XB], ["xm%d" % xi])
                            rs = rms_stats(xm[0:pp, xi, :], D, pp, 8 + 4 * b, "xm%d" % xi, "n2_%d" % b, junkb[0:pp, :])
                            ts("dve", xn2[b][0:pp, :], xm[0:pp, xi, :], rs, None, ALU.mult, None, ["xm%d" % xi, "n2_%drs" % b], [XN2])

                        def x2():
                            tr([(PTb[:, c, 0:pp], xn2[b][0:pp, c * 128:(c + 1) * 128], ident[0:pp, 0:pp]) for c in range(8)],
                               [XN2, "ident"], [PTr])
                            if pp == 1:
                                side = 0 if xi == 0 else 1
                                ts("dve", h2T[:, :, qc:qc + 1], PTb[:, :, 0:1], masks[:, seg * 2 + side:seg * 2 + side + 1], None,
                                   ALU.mult, None, [PTr, "masks"], ["h2T"])
                            else:
                                cp("dve", h2T[:, :, qc:qc + 128], PTb, [PTr], ["h2T"])

                        return (x1, x2)

                    xl = [xm_stages(ti, *t_) for ti, t_ in enumerate(tiles)]
                    if PIPE:
                        for s_ in range(len(xl) + 1):
                            if s_ < len(xl):
                                xl[s_][0]()
                            if s_ >= 1:
                                xl[s_ - 1][1]()
                    else:
                        for x1_, x2_ in xl:
                            x1_()
                            x2_()

                    cpv = convp[:].rearrange("p (j s k) -> p j s k", j=NJ, s=2)
                    ucount = 0
                    deferred = []
                    for ck in range(2):
                        for j in range(NJ):
                            wb_ = j % 2
                            if not (ck == 1 and j < 2):
                                dma("wu%d" % wb_, wu[wb_][:], s_w_up[j], ["scr"], ["wu%d" % wb_])
                            b = ucount % 2
                            ucount += 1
                            gi_, vi_ = 2 * b, 2 * b + 1
                            PG, PV_ = P[gi_], P[vi_]
                            for sub in range(2):
                                c0 = ck * 512 + sub * 256
                                lst = []
                                for c in range(8):
                                    lst.append((PG[:, sub * 512:sub * 512 + 258], wu[wb_][:, c, 0:128], h2T[:, c, c0:c0 + 258], c == 0, c == 7))
                                for c in range(8):
                                    lst.append((PV_[:, sub * 512:sub * 512 + 258], wu[wb_][:, c, 128:256], h2T[:, c, c0:c0 + 258], c == 0, c == 7))
                                mm(lst, ["wu%d" % wb_, "h2T"], ["P%d%s" % (gi_, "ab"[sub]), "P%d%s" % (vi_, "ab"[sub])])
                            G3 = PG[:, :].rearrange("p (s c) -> p s c", s=2)
                            V3 = PV_[:, :].rearrange("p (s c) -> p s c", s=2)
                            ga3 = ga[b][:, :].rearrange("p (s c) -> p s c", s=2)
                            va3 = va[b][:, :].rearrange("p (s c) -> p s c", s=2)
                            for s_, (acc, nm, src3_, pr_) in enumerate(((ga3, "ga%d" % b, G3, "P%d" % gi_), (va3, "va%d" % b, V3, "P%d" % vi_))):
                                act(acc, src3_[:, :, 0:256], AF.Identity, [pr_, "convp"], [nm],
                                    scale=cpv[:, j, s_, 0:1], bias=cpv[:, j, s_, 3:4])
                            tv3 = tv[b][:, :].rearrange("p (s c) -> p s c", s=2)
                            act(tv3, V3[:, :, 2:258], AF.Identity, ["P%d" % vi_, "convp"], ["tv%d" % b], scale=cpv[:, j, 1, 2:3])
                            stt("dve", ga3, G3[:, :, 1:257], cpv[:, j, 0, 1:2], ga3, ALU.mult, ALU.add,
                                ["P%d" % gi_, "convp", "ga%d" % b], ["ga%d" % b])
                            stt("dve", ga3, G3[:, :, 2:258], cpv[:, j, 0, 2:3], ga3, ALU.mult, ALU.add,
                                ["P%d" % gi_, "convp", "ga%d" % b], ["ga%d" % b])
                            stt("dve", va3, V3[:, :, 1:257], cpv[:, j, 1, 1:2], va3, ALU.mult, ALU.add,
                                ["P%d" % vi_, "convp", "va%d" % b], ["va%d" % b])
                            act(sg[b][:], ga[b][:], AF.Silu, ["ga%d" % b], ["sg%d" % b])
                            tt("pool", va[b][:], va[b][:], tv[b][:], ALU.add, ["va%d" % b, "tv%d" % b], ["va%d" % b])
                            tt("pool", actT[:, j, :], sg[b][:], va[b][:], ALU.mult,
                               ["sg%d" % b, "va%d" % b], ["actT%d" % j])
                            if j == 18:
                                for jg_ in range(2):
                                    j0_, nj_ = ((0, 4), (4, 4))[jg_]
                                    dma("wd%d" % jg_, wd[jg_][:, 0:nj_, :], s_w_down[j0_ // 4][:, 0:nj_, :], ["scr"], ["wd%d" % jg_])
                            if deferred and j >= 2 and j % 2 == 0:
                                deferred.pop(0)()
                        if ck == 0:
                            for j_ in range(2):
                                dma("wu%d" % j_, wu[j_][:], s_w_up[j_], ["scr"], ["wu%d" % j_])
                        for hf in range(2):
                            grp = [(0, 4), (4, 4), (8, 4), (12, 4), (16, 4), (20, 2)] if hf == 0 else \
                                  [(0, 2), (2, 4), (6, 4), (10, 4), (14, 4), (18, 4)]
                            for jg, (j0, nj) in enumerate(grp):
                                wb_ = jg % 2
                                idx0 = hf * NJ + j0
                                if not (hf == 0 and jg < 2):
                                    dma("wd%d" % wb_, wd[wb_][:, 0:nj, :], s_w_down[idx0 // 4][:, idx0 % 4:idx0 % 4 + nj, :],
                                        ["scr"], ["wd%d" % wb_])
                                for jj in range(nj):
                                    j = j0 + jj
                                    lst = []
                                    for m in range(4):
                                        lst.append((P[2 * hf + m // 2][:, (m % 2) * 512:(m % 2) * 512 + 512],
                                                    actT[:, j, m * 128:(m + 1) * 128], wd[wb_][:, jj, :], j == 0, j == NJ - 1))
                                    mm(lst, ["actT%d" % j, "wd%d" % wb_], ["P%d" % (2 * hf), "P%d" % (2 * hf + 1)])
                            for m in range(4):
                                xi = 1 + ck * 4 + m
                                tt("dve", xm[:, xi, hf * 512:(hf + 1) * 512], P[2 * hf + m // 2][:, (m % 2) * 512:(m % 2) * 512 + 512],
                                   xm[:, xi, hf * 512:(hf + 1) * 512], ALU.add, ["P%d%s" % (2 * hf + m // 2, "ab"[m % 2]), "xm%d" % xi], ["xm%d" % xi])
                        for m in range(4):
                            xi = 1 + ck * 4 + m
                            act(junkb[:], xm[:, xi, :], AF.Square, ["xm%d" % xi], ["n3ms%d" % m, "junkB"],
                                scale=float(D) ** -0.5, accum=stat[:, 20 + m:21 + m])
                        act(stat[:, 24:28], stat[:, 20:24], AF.Ln, ["n3ms%d" % m for m in range(4)] + ["epsb"], ["n3ln"], bias=epsb[:, :])
                        act(stat[:, 28:32], stat[:, 24:28], AF.Exp, ["n3ln"], ["n3rs"], scale=-0.5)
                        for m in range(4):
                            xi = 1 + ck * 4 + m
                            ob = m % 2
                            stt("dve", yo[ob][:], xm[:, xi, :], stat[:, 28 + m:29 + m], gfin[:], ALU.mult, ALU.mult,
                                ["xm%d" % xi, "n3rs", "gfin"], ["yo%d" % ob])
                            dma("yo%d" % ob, y_out[seg, (xi - 1) * 128:xi * 128, :], yo[ob][:], ["yo%d" % ob], ["yout"])
                    with nc.Block() as blk:
                        S.flush(blk)
                        if seg == NSEG - 1:
                            pass
        with nc.Block() as blk:
            S.final_wait(blk)
    return nc


def _rope_tab(pos):
    inv = (10000.0 ** (-np.arange(0, 32, 2, dtype=np.float32) / 32.0)).astype(np.float32)
    ang = pos.astype(np.float32)[:, None] * inv[None, :]
    c, s_ = np.cos(ang).astype(np.float32), np.sin(ang).astype(np.float32)
    return np.concatenate([c, c, -s_, s_], axis=1).astype(np.float32)


def _band(kind):
    wins = (2, 4, 8, 16)
    if kind == "c":
        out = np.zeros((3, 4, 128, 128), np.float32)
        for g, w in enumerate(wins):
            hw = w // 2
            for t in range(128):
                ta = 128 + t
                for tp in range(ta - hw, ta + hw):
                    out[tp // 128, g, tp % 128, t] += 1.0 / w
                out[1, g, t, t] -= 1.0
        return out
    out = np.zeros((4, 128, 128), np.float32)
    for g, w in enumerate(wins):
        hw = w // 2
        for t in range(128):
            lo, hi = t - hw, t + hw
            if kind == "f":
                lo = max(lo, 0)
            else:
                hi = min(hi, 128)
            cnt = hi - lo
            for tp in range(max(lo, 0), min(hi, 128)):
                out[g, tp, t] += 1.0 / cnt
            out[g, t, t] -= 1.0
    return out


_NC_CACHE = {}


def kernel(x_prompt, x_sample, norm_mix_g, w_in, q_norm_g, w_uq, kv_norm_g, w_ukv, w_pool, pool_scale,
           w_out, norm_ffn_g, w_up, conv_w, conv_b, w_down, final_norm_g):
    f = np.float32
    x_prompt = np.asarray(x_prompt, f)
    x_sample = np.asarray(x_sample, f)

    def pk(w, k):
        w = np.asarray(w, f)
        return np.ascontiguousarray(w.reshape(k, 128, -1).transpose(1, 0, 2))

    def pv(v, k):
        return np.ascontiguousarray(np.asarray(v, f).reshape(k, 128).T)

    w_up0 = np.asarray(w_up, f)[0]
    wu = np.empty((NJ, 128, 8, 256), f)
    for j in range(NJ):
        wu[j, :, :, 0:128] = w_up0[:, j * 128:(j + 1) * 128].reshape(8, 128, 128).transpose(1, 0, 2)
        wu[j, :, :, 128:256] = w_up0[:, DFF + j * 128:DFF + (j + 1) * 128].reshape(8, 128, 128).transpose(1, 0, 2)
    w_down0 = np.asarray(w_down, f)[0]
    wdn = np.empty((11, 128, 4, 512), f)
    for hf in range(2):
        for j in range(NJ):
            idx = hf * NJ + j
            wdn[idx // 4, :, idx % 4, :] = w_down0[j * 128:(j + 1) * 128, hf * 512:(hf + 1) * 512]
    cw = np.asarray(conv_w, f)[0]
    cb = np.asarray(conv_b, f)[0]
    convp = np.empty((128, NJ, 2, 4), f)
    for j in range(NJ):
        for s_, off in enumerate((0, DFF)):
            sl = slice(off + j * 128, off + (j + 1) * 128)
            convp[:, j, s_, 0] = cw[0, sl]
            convp[:, j, s_, 1] = cw[1, sl]
            convp[:, j, s_, 2] = cw[2, sl]
            convp[:, j, s_, 3] = cb[sl]
    bc = _band("c")
    band_c = np.ascontiguousarray(bc.reshape(12, 128, 128).transpose(1, 0, 2))
    bcur = bc[1]
    bfirst, blast = _band("f"), _band("l")
    shared = {
        "band_c": band_c,
        "w_in_h": pk(np.asarray(w_in, f)[0], 8), "g_mix": pv(np.asarray(norm_mix_g, f)[0], 8),
        "w_uq_h": pk(np.asarray(w_uq, f)[0], 2), "g_q": pv(np.asarray(q_norm_g, f)[0], 2),
        "w_ukv_h": pk(np.asarray(w_ukv, f)[0], 1), "g_kv": pv(np.asarray(kv_norm_g, f)[0], 1),
        "w_pool_h": np.ascontiguousarray(np.asarray(w_pool, f)[0].transpose(1, 0, 2)),
        "w_out_h": pk(np.asarray(w_out, f)[0], 8), "pscale": pv(np.asarray(pool_scale, f)[0], 4),
        "w_up_h": wu, "g_ffn": pv(np.asarray(norm_ffn_g, f)[0], 8), "w_down_h": wdn,
        "convp": np.ascontiguousarray(convp.reshape(128, NJ * 8)),
        "gfin": np.ascontiguousarray(np.broadcast_to(np.asarray(final_norm_g, f)[None, :], (128, D))),
        "ident": np.eye(128, dtype=f),
    }

    def ext_rows(seq, s0):
        S_ = seq.shape[0]
        out = np.zeros((EXT, D), f)
        lo, hi = s0 - 128, s0 + 1024 + 128
        a, b = max(lo, 0), min(hi, S_)
        out[a - lo:b - lo] = seq[a:b]
        return out

    in_maps = []
    seginfo = []
    for c in range(8):
        groups = [(x_prompt[2 * c], 0), (x_prompt[2 * c + 1], 0), (x_sample[c // 4], (c % 4) * 2048)]
        xext = np.empty((NSEG, EXT, D), f)
        ropeq = np.empty((NSEG, 128, 10, 64), f)
        masks = np.zeros((NSEG, 2), f)
        band_f = np.empty((NSEG, 128, 4, 128), f)
        band_l = np.empty((NSEG, 128, 4, 128), f)
        for gi, (seq, gs) in enumerate(groups):
            S_ = seq.shape[0]
            for half in range(2):
                seg = gi * 2 + half
                s0 = gs + half * 1024
                xext[seg] = ext_rows(seq, s0)
                pos = np.arange(s0 - 128, s0 + 1152)
                ropeq[seg] = _rope_tab(pos).reshape(10, 128, 64).transpose(1, 0, 2)
                masks[seg, 0] = 1.0 if s0 - 1 >= 0 else 0.0
                masks[seg, 1] = 1.0 if s0 + 1024 < S_ else 0.0
                band_f[seg] = (bfirst if s0 == 0 else bcur).transpose(1, 0, 2)
                band_l[seg] = (blast if s0 + 1024 == S_ else bcur).transpose(1, 0, 2)
        xoth_p = np.stack([x_prompt[2 * c][1024:2048], x_prompt[2 * c + 1][1024:2048]])
        ropek_p = np.stack([_rope_tab(np.arange(1024, 2048)).reshape(8, 128, 64)] * 2)
        xs = x_sample[c // 4]
        gs = (c % 4) * 2048
        opos = np.concatenate([np.arange(gs + 1024, gs + 2048), np.arange(0, gs), np.arange(gs + 2048, 8192)])
        xoth_s = np.ascontiguousarray(xs[opos])
        ropek_s = _rope_tab(opos).reshape(56, 128, 64)
        m = dict(shared)
        m.update({
            "xext": xext, "xoth_p": np.ascontiguousarray(xoth_p), "xoth_s": xoth_s,
            "ropeq": ropeq, "ropek_p": np.ascontiguousarray(ropek_p), "ropek_s": np.ascontiguousarray(ropek_s),
            "masks": np.ascontiguousarray(np.broadcast_to(masks.reshape(1, 12), (128, 12))),
            "band_f": band_f, "band_l": band_l,
        })
        in_maps.append(m)

    if "nc" not in _NC_CACHE:
        _NC_CACHE["nc"] = build_program()
    nc = _NC_CACHE["nc"]
    res = run_bass_kernel_spmd(nc, in_maps, core_ids=list(range(8)))
    y_prompt = np.empty((16, 2048, D), f)
    y_sample = np.empty((2, 8192, D), f)
    for c in range(8):
        y = np.asarray(res.results[c]["y"], f).reshape(NSEG, 1024, D)
        y_prompt[2 * c] = y[0:2].reshape(2048, D)
        y_prompt[2 * c + 1] = y[2:4].reshape(2048, D)
        gs = (c % 4) * 2048
        y_sample[c // 4, gs:gs + 2048] = y[4:6].reshape(2048, D)
    return (y_prompt, y_sample)
```

```python
import math
from contextlib import ExitStack

import numpy as np
import concourse.bass as bass
import concourse.mybir as mybir
from concourse.bass_utils import run_bass_kernel_spmd

F32 = mybir.dt.float32
BF16 = mybir.dt.bfloat16
AF = mybir.ActivationFunctionType
ALU = mybir.AluOpType

D = 1024
NSEG = 6
EXT = 1280
DFF = 2816
NJ = 22
SC = 1.0 / math.sqrt(96.0)
PIPE = True
ATT_PIPE = True
ACT_COPY = False
PREP_ACT = True
BG_POOL = True
PREP_EARLY = True
COMPUTE = ("act", "pool", "dve", "pe")
ENGS = ("sync", "act", "pool", "dve", "pe")


class Op:
    __slots__ = ("eng", "fn", "deps", "dma_key", "dma_cnt", "signal", "sig_idx", "blk")


class Sched:
    def __init__(self, nc, es):
        self.nc = nc
        self.es = es
        self.pending = []
        self.lastw = {}
        self.readers = {}
        self.sem = {e: es.enter_context(nc.semaphore("c_" + e)) for e in COMPUTE}
        self.count = {e: 0 for e in COMPUTE}
        self.dma_sem = {}
        self.dma_cnt = {}
        self.dma_last = {}
        self.waited = {e: {} for e in ENGS}
        self.blk = 0
        self.log = {e: [] for e in ENGS}
        self.bank_last = {}

    EXPAND = {"P0": ("P0a", "P0b"), "P1": ("P1a", "P1b"), "P2": ("P2a", "P2b"), "P3": ("P3a", "P3b"),
              "PO": ("PO", "P2a", "P2b"),
              "PY": ("P2a", "P2b", "P3a", "P3b"), "P2_0": ("P2a",), "P2_1": ("P2b",),
              "P3_0": ("P3a",), "P3_1": ("P3b",), "POH": ("POH", "P3b")}
    BANKS = frozenset(("P0a", "P0b", "P1a", "P1b", "P2a", "P2b", "P3a", "P3b"))

    def _expand(self, names):
        out = []
        for n in names:
            out.extend(self.EXPAND.get(n, (n,)))
        return out

    def op(self, eng, fn, reads=(), writes=(), dma=None):
        o = Op()
        o.eng, o.fn, o.dma_key, o.signal, o.sig_idx, o.blk = eng, fn, dma, False, 0, self.blk
        reads = self._expand(reads)
        writes = self._expand(writes)
        deps = set()
        for r in set(reads) | set(writes):
            if r in self.BANKS:
                bl = self.bank_last.setdefault(r, {})
                for e2, o2 in bl.items():
                    if e2 != eng:
                        deps.add(o2)
                bl[eng] = o
        for r in reads:
            w = self.lastw.get(r)
            if w is not None:
                deps.add(w)
        for r in writes:
            w = self.lastw.get(r)
            if w is not None:
                deps.add(w)
            deps.update(self.readers.get(r, ()))
        if dma is not None:
            p = self.dma_last.get(dma)
            if p is not None:
                deps.add(p)
            self.dma_last[dma] = o
            if dma not in self.dma_sem:
                self.dma_sem[dma] = self.es.enter_context(self.nc.semaphore("d_" + dma))
                self.dma_cnt[dma] = 0
            self.dma_cnt[dma] += 1
            o.dma_cnt = self.dma_cnt[dma]
        else:
            o.dma_cnt = 0
        deps.discard(o)
        o.deps = deps
        for r in reads:
            self.readers.setdefault(r, []).append(o)
        for r in writes:
            self.lastw[r] = o
            self.readers[r] = []
        self.pending.append(o)
        return o

    def flush(self, block):
        ops = self.pending
        self.pending = []
        blk = self.blk
        for o in ops:
            for d in o.deps:
                if d.blk != blk or d.dma_key is not None:
                    continue
                if d.eng == "pe" and o.eng == "pe":
                    continue
                d.signal = True
        last = {}
        for o in ops:
            if o.dma_key is None:
                last[o.eng] = o
        for o in last.values():
            o.signal = True
        start_counts = dict(self.count)
        start_dma = {k: v for k, v in self.dma_cnt.items()}
        for o in ops:
            if o.dma_key is not None:
                start_dma[o.dma_key] = min(start_dma[o.dma_key], o.dma_cnt - 1)
        for o in ops:
            if o.dma_key is None and o.signal:
                self.count[o.eng] += 1
                o.sig_idx = self.count[o.eng]

        def body_for(eng):
            def body(e):
                wd = self.waited[eng]

                lg = self.log[eng]

                def wait(key, sem, val):
                    if val > 0 and wd.get(key, 0) < val:
                        e.wait_ge(sem, val)
                        wd[key] = val
                        lg.append(("w", key, val))

                for ce in COMPUTE:
                    wait(("c", ce), self.sem[ce], start_counts[ce])
                for k, c in start_dma.items():
                    wait(("d", k), self.dma_sem[k], 16 * c)
                for o in ops:
                    if o.eng != eng:
                        continue
                    for d in o.deps:
                        if d.blk != blk:
                            continue
                        if d.dma_key is not None:
                            wait(("d", d.dma_key), self.dma_sem[d.dma_key], 16 * d.dma_cnt)
                        elif d.eng == "pe" and eng == "pe":
                            continue
                        else:
                            wait(("c", d.eng), self.sem[d.eng], d.sig_idx)
                    ins = o.fn(e)
                    if o.dma_key is not None:
                        ins.then_inc(self.dma_sem[o.dma_key], 16)
                        lg.append(("i", ("d", o.dma_key), 16))
                    elif o.signal:
                        ins.then_inc(self.sem[eng], 1)
                        lg.append(("i", ("c", eng), 1))
            return body

        block.sync(body_for("sync"))
        block.scalar(body_for("act"))
        block.gpsimd(body_for("pool"))
        block.vector(body_for("dve"))
        block.tensor(body_for("pe"))
        self.blk += 1

    def final_wait(self, block):
        def body(e):
            for ce in COMPUTE:
                if self.count[ce] > 0:
                    e.wait_ge(self.sem[ce], self.count[ce])
            for k, c in self.dma_cnt.items():
                e.wait_ge(self.dma_sem[k], 16 * c)
        block.sync(body)


def build_program():
    nc = bass.Bass("TRN2", target_bir_lowering=False)

    def din(name, shape):
        return nc.dram_tensor(name, list(shape), F32, kind="ExternalInput").ap()

    xext = din("xext", (NSEG, EXT, D))
    xoth_p = din("xoth_p", (2, 1024, D))
    xoth_s = din("xoth_s", (7168, D))
    ropeq = din("ropeq", (NSEG, 128, 10, 64))
    ropek_p = din("ropek_p", (2, 8, 128, 64))
    ropek_s = din("ropek_s", (56, 128, 64))
    masks_d = din("masks", (128, 12))
    band_c_d = din("band_c", (128, 12, 128))
    band_f_d = din("band_f", (NSEG, 128, 4, 128))
    band_l_d = din("band_l", (NSEG, 128, 4, 128))
    w_in_d = din("w_in_h", (128, 8, 928))
    g_mix_d = din("g_mix", (128, 8))
    w_uq_d = din("w_uq_h", (128, 2, 768))
    g_q_d = din("g_q", (128, 2))
    w_ukv_d = din("w_ukv_h", (128, 1, 1024))
    g_kv_d = din("g_kv", (128, 1))
    w_pool_d = din("w_pool_h", (128, 4, 128))
    w_out_d = din("w_out_h", (128, 8, 1024))
    pscale_d = din("pscale", (128, 4))
    w_up_d = din("w_up_h", (NJ, 128, 8, 256))
    g_ffn_d = din("g_ffn", (128, 8))
    w_down_d = din("w_down_h", (11, 128, 4, 512))
    convp_d = din("convp", (128, NJ * 8))
    gfin_d = din("gfin", (128, D))
    ident_d = din("ident", (128, 128))
    y_out = nc.dram_tensor("y", [NSEG, 1024, D], F32, kind="ExternalOutput").ap()

    def dscr(name, shape):
        return nc.dram_tensor(name, list(shape), BF16, kind="Internal").ap()

    s_w_in = dscr("s_w_in", (128, 8, 928))
    s_w_uq = dscr("s_w_uq", (128, 2, 768))
    s_w_ukv = dscr("s_w_ukv", (128, 1, 1024))
    s_w_pool = dscr("s_w_pool", (128, 4, 128))
    s_w_out = dscr("s_w_out", (128, 8, 1024))
    s_w_up = dscr("s_w_up", (NJ, 128, 8, 256))
    s_w_down = dscr("s_w_down", (11, 128, 4, 512))
    s_band_c = dscr("s_band_c", (128, 12, 128))
    recd = nc.dram_tensor("recd", [3, 512], F32, kind="Internal").ap()

    with ExitStack() as es:
        E = es.enter_context

        uid = [0]

        def sb(name, shape, dt, st=None):
            uid[0] += 1
            return (st or es).enter_context(nc.sbuf_tensor("sb%d_%s" % (uid[0], name), list(shape), dt))

        S = Sched(nc, es)
        P = [E(nc.psum_tensor("P%d" % i, [128, 1024], F32)) for i in range(4)]

        ident_f = sb("ident_f", (128, 128), F32)
        ident = sb("ident_b", (128, 128), BF16)
        epsb = sb("epsb", (128, 1), F32)
        ones_f = sb("ones_f", (128, 128), F32)
        w_out_b = sb("w_out_b", (128, 8, 1024), BF16)
        mixP = sb("mixP", (128, 4, 1026), BF16)
        attnT = sb("attnT", (128, 4, 1026), BF16)
        kv_nT = sb("kv_nT", (128, 8192), BF16)
        kT = sb("kT", (96, 8192), BF16)
        gfin = sb("gfin", (128, D), F32)
        convp = sb("convp", (128, NJ * 8), F32)
        masks = sb("masks", (128, 12), F32)
        gsm = sb("gsm", (128, 32), F32)
        stat = sb("stat", (128, 64), F32)

        def dma(key, out, in_, reads, writes):
            return S.op("sync", lambda e: e.dma_start(out=out, in_=in_), reads, writes, dma=key)

        def act(out, in_, func, reads, writes, scale=1.0, bias=None, accum=None):
            def fn(e):
                kw = {}
                if bias is not None:
                    kw["bias"] = bias
                if accum is not None:
                    kw["accum_out"] = accum
                return e.activation(out=out, in_=in_, func=func, scale=scale, **kw)
            return S.op("act", fn, reads, writes)

        def ts(eng, out, in0, s1, s2, op0, op1, reads, writes):
            if s2 is None:
                return S.op(eng, lambda e: e.tensor_scalar(out=out, in0=in0, scalar1=s1, scalar2=None, op0=op0), reads, writes)
            return S.op(eng, lambda e: e.tensor_scalar(out=out, in0=in0, scalar1=s1, scalar2=s2, op0=op0, op1=op1), reads, writes)

        def tt(eng, out, in0, in1, op, reads, writes):
            return S.op(eng, lambda e: e.tensor_tensor(out=out, in0=in0, in1=in1, op=op), reads, writes)

        def stt(eng, out, in0, scalar, in1, op0, op1, reads, writes):
            return S.op(eng, lambda e: e.scalar_tensor_tensor(out=out, in0=in0, scalar=scalar, in1=in1, op0=op0, op1=op1), reads, writes)

        def cp(eng, out, in_, reads, writes):
            return S.op(eng, lambda e: e.tensor_copy(out=out, in_=in_), reads, writes)

        def acp(out, in_, reads, writes):
            if ACT_COPY:
                return act(out, in_, AF.Identity, reads, writes)
            return cp("dve", out, in_, reads, writes)

        def mm(lst, reads, writes):
            def fn(e):
                ins = None
                for (o, l, r, st, sp) in lst:
                    ins = e.matmul(o, lhsT=l, rhs=r, start=st, stop=sp)
                return ins
            return S.op("pe", fn, reads, writes)

        def tr(lst, reads, writes):
            def fn(e):
                ins = None
                for (o, i, idn) in lst:
                    ins = e.transpose(out=o, in_=i, identity=idn)
                return ins
            return S.op("pe", fn, reads, writes)

        def rms_stats(src, n, pp, col, rsrc, tag, junk):
            ms = stat[0:pp, col:col + 1]
            ln = stat[0:pp, col + 1:col + 2]
            rs = stat[0:pp, col + 2:col + 3]
            act(junk, src, AF.Square, [rsrc], [tag + "ms"], scale=float(n) ** -0.5, accum=ms)
            act(ln, ms, AF.Ln, [tag + "ms", "epsb"], [tag + "ln"], bias=epsb[0:pp, :])
            act(rs, ln, AF.Exp, [tag + "ln"], [tag + "rs"], scale=-0.5)
            return rs

        with ExitStack() as st0:
            stg32 = [sb("stg32_%d" % i, (128, 2048), F32, st0) for i in range(2)]
            stg16 = [sb("stg16_%d" % i, (128, 2048), BF16, st0) for i in range(2)]
            S.op("dve", lambda e: e.memset(epsb[:], 1e-6), [], ["epsb"])
            S.op("dve", lambda e: e.memset(ones_f[:], 1.0), [], ["ones_f"])
            dma("c0", ident_f[:], ident_d, [], ["ident_f"])
            cp("dve", ident[:], ident_f[:], ["ident_f"], ["ident"])
            dma("c1", gfin[:], gfin_d, [], ["gfin"])
            dma("c2", convp[:], convp_d, [], ["convp"])
            dma("c3", masks[:], masks_d, [], ["masks"])
            dma("c4", gsm[:, 0:8], g_mix_d, [], ["gsm"])
            dma("c5", gsm[:, 8:10], g_q_d, [], ["gsm"])
            dma("c6", gsm[:, 10:11], g_kv_d, [], ["gsm"])
            dma("c7", gsm[:, 11:15], pscale_d, [], ["gsm"])
            dma("c8", gsm[:, 16:24], g_ffn_d, [], ["gsm"])
            cnt = [0]

            def conv(src, dst, a, bdim, scale_ap):
                b = cnt[0] % 2
                cnt[0] += 1
                n = a * bdim
                v32 = stg32[b][:, 0:n].rearrange("p (a b) -> p a b", a=a)
                v16 = stg16[b][:, 0:n].rearrange("p (a b) -> p a b", a=a)
                dma("ld32_%d" % b, v32, src, [], ["stg32_%d" % b])
                if scale_ap is None:
                    cp("dve", v16, v32, ["stg32_%d" % b], ["stg16_%d" % b])
                else:
                    tt("dve", v16, v32, scale_ap.unsqueeze(2).to_broadcast([128, a, bdim]), ALU.mult,
                       ["stg32_%d" % b, "gsm"], ["stg16_%d" % b])
                dma("st16_%d" % b, dst, v16, ["stg16_%d" % b], ["scr"])

            for c in range(4):
                conv(w_in_d[:, 2 * c:2 * c + 2, :], s_w_in[:, 2 * c:2 * c + 2, :], 2, 928, gsm[:, 2 * c:2 * c + 2])
            conv(w_uq_d, s_w_uq, 2, 768, gsm[:, 8:10])
            conv(w_ukv_d, s_w_ukv, 1, 1024, gsm[:, 10:11])
            conv(w_pool_d, s_w_pool, 4, 128, None)
            for c in range(4):
                conv(w_out_d[:, 2 * c:2 * c + 2, :], s_w_out[:, 2 * c:2 * c + 2, :], 2, 1024,
                     gsm[:, 11 + 2 * c:13 + 2 * c] if c < 2 else None)
            conv(band_c_d, s_band_c, 12, 128, None)
            dma("c9", w_out_b[:], s_w_out, ["scr"], ["w_out_b"])
            with nc.Block() as blk:
                S.flush(blk)

        groups = [("p", 0, 2048), ("p", 1, 2048), ("s", 0, 8192)]
        for gi, (gkind, gidx, TK) in enumerate(groups):
            NKT = TK // 128
            for half in range(2):
                seg = gi * 2 + half
                with ExitStack() as sa:
                    w_in_b = sb("w_in_b", (128, 8, 928), BF16, sa)
                    w_uq_b = sb("w_uq_b", (128, 2, 768), BF16, sa)
                    w_ukv_b = sb("w_ukv_b", (128, 1024), BF16, sa)
                    w_pool_b = sb("w_pool_b", (128, 4, 128), BF16, sa)
                    band_c = sb("band_c", (128, 12, 128), BF16, sa)
                    band_fl32 = sb("band_fl32", (128, 8, 128), F32, sa)
                    band_fl = sb("band_fl", (128, 8, 128), BF16, sa)
                    rq = sb("rq", (128, 10, 64), F32, sa)
                    rk = [sb("rk%d" % i, (128, 64), F32, sa) for i in range(4)]
                    upool = sb("upool", (128, 10, 512), BF16, sa)
                    q_nT = sb("q_nT", (128, 2, EXT), BF16, sa)
                    Vx = sb("Vx", (128, 64, 128), BF16, sa)
                    qT = sb("qT", (96, 8, 1026), BF16, sa)
                    NX = 2 if seg == 0 else 3
                    xt = [sb("xt%d" % i, (128, D), F32, sa) for i in range(NX)]
                    xn = [sb("xn%d" % i, (128, D), BF16, sa) for i in range(2)]
                    hT = [sb("hT%d" % i, (128, 8, 128), BF16, sa) for i in range(2)]
                    junk = sb("junkA", (128, D), BF16, sa)
                    stage = [sb("stage%d" % i, (128, 480), BF16, sa) for i in range(2)]
                    rtmp = [sb("rtmp%d" % i, (128, 8, 96), F32, sa) for i in range(2)]
                    qtm = [sb("qtm%d" % i, (128, 768), BF16, sa) for i in range(2)]
                    pooled = [sb("pooled%d" % i, (128, 4, 128), BF16, sa) for i in range(2)]
                    PTs = [sb("PTs%d" % i, (128, 1024), BF16, sa) for i in range(3)]
                    PHs = [sb("PHs%d" % i, (128, 32), BF16, sa) for i in range(2)]
                    otmp = [sb("otmp%d" % i, (128, 512), F32, sa) for i in range(2)]
                    rec = [sb("rec%d" % i, (128, 512), F32, sa) for i in range(2)]
                    ohacc = sb("ohacc", (128, 2), F32, sa)
                    oth = sb("oth", (128, 2), F32, sa)
                    rech = sb("rech", (128, 2), F32, sa)

                    dma("x0", xt[0][:], xext[seg, 0:128, :], [], ["xt0"])
                    dma("a0", w_in_b[:], s_w_in, ["scr"], ["w_in_b"])
                    dma("a7", rq[:], ropeq[seg], [], ["rq"])
                    dma("a1", w_uq_b[:], s_w_uq, ["scr"], ["w_uq_b"])
                    dma("a2", w_ukv_b[:], s_w_ukv[:, 0, :], ["scr"], ["w_ukv_b"])
                    dma("a3", w_pool_b[:], s_w_pool, ["scr"], ["w_pool_b"])
                    dma("a4", band_c[:], s_band_c, ["scr"], ["band_c"])
                    dma("a5", band_fl32[:, 0:4, :], band_f_d[seg], [], ["band_fl32"])
                    dma("a6", band_fl32[:, 4:8, :], band_l_d[seg], [], ["band_fl32"])
                    for i in range(2):
                        S.op("pool", lambda e, t=stage[i]: e.memset(t[:, 384:448], 0.0), [], ["stagez%d" % i])

                    def p1_stages(t, src_ap, full, kcol, rope_ap, rope_res, ext_i, pre=None):
                        b = t % 2
                        xb_ = t % NX
                        X, XN, HT, ST = "xt%d" % xb_, "xn%d" % b, "hT%d" % b, "stage%d" % b
                        PTb = P[3][:, b * 512:(b + 1) * 512].bitcast(BF16).rearrange("p (a b) -> p a b", a=8)
                        PSb = P[2][:, b * 512:(b + 1) * 512].bitcast(BF16).rearrange("p (a b) -> p a b", a=8)
                        PZ = P[b]
                        PTr, PSr, PZr = "P3_%d" % b, "P2_%d" % b, "P%d" % b

                        def stA():
                            if pre is not None:
                                pre()
                            if t != 0:
                                dma("x%d" % xb_, xt[xb_][:], src_ap, [], [X])
                            rs = rms_stats(xt[xb_][:], D, 128, 0 + 16 * b, X, "n1_%d" % b, junk[:])
                            ts("dve", xn[b][:], xt[xb_][:], rs, None, ALU.mult, None, [X, "n1_%drs" % b], [XN])

                        def stB():
                            tr([(PTb[:, c, :], xn[b][:, c * 128:(c + 1) * 128], ident[:]) for c in range(8)],
                               [XN, "ident"], [PTr])
                            cp("dve", hT[b][:], PTb, [PTr], [HT])

                        def stC():
                            lst = []
                            if full:
                                for c in range(8):
                                    lst.append((PZ[:, 0:512], hT[b][:, c, :], w_in_b[:, c, 0:512], c == 0, c == 7))
                            z0 = 512 if full else 768
                            for c in range(8):
                                lst.append((PZ[:, z0:928], hT[b][:, c, :], w_in_b[:, c, z0:928], c == 0, c == 7))
                            mm(lst, [HT, "w_in_b"], [PZr])
                            Ba, Bb = "P%da" % b, "P%db" % b
                            rq_ = rk_ = None
                            if full:
                                rq_ = rms_stats(PZ[:, 512:768], 256, 128, 4 + 16 * b, Bb, "nq_%d" % b, junk[:, 0:256])
                            if kcol is not None:
                                rk_ = rms_stats(PZ[:, 768:896], 128, 128, 8 + 16 * b, Bb, "nk_%d" % b, junk[:, 0:128])
                            if full:
                                act(upool[:, ext_i, :], PZ[:, 0:512], AF.Identity, [Ba], ["upool%d" % ext_i])
                                ts("dve", stage[b][:, 0:256], PZ[:, 512:768], rq_, None, ALU.mult, None,
                                   [Bb, "nq_%drs" % b], [ST + "q"])
                            if kcol is not None:
                                ts("dve", stage[b][:, 256:384], PZ[:, 768:896], rk_, None, ALU.mult, None,
                                   [Bb, "nk_%drs" % b], [ST + "k"])
                                rt = rtmp[b]
                                RT = "rtmp%d" % b
                                tt("dve", rt[:, 0, 0:32], PZ[:, 896:928], rope_ap[:, 0:32], ALU.mult, [Bb, rope_res], [RT + "a"])
                                tt("dve", rt[:, 0, 32:48], PZ[:, 912:928], rope_ap[:, 32:48], ALU.mult, [Bb, rope_res], [RT + "b"])
                                tt("dve", rt[:, 0, 48:64], PZ[:, 896:912], rope_ap[:, 48:64], ALU.mult, [Bb, rope_res], [RT + "c"])
                                tt("dve", stage[b][:, 448:480], rt[:, 0, 0:32], rt[:, 0, 32:64], ALU.add,
                                   [RT + "a", RT + "b", RT + "c"], [ST + "kr"])

                        def stD():
                            lst = []
                            rd = ["ident"]
                            if full:
                                lst += [(PSb[:, 0, :], stage[b][:, 0:128], ident[:]), (PSb[:, 1, :], stage[b][:, 128:256], ident[:])]
                                rd.append(ST + "q")
                            if kcol is not None:
                                lst += [(PSb[:, 2, :], stage[b][:, 256:384], ident[:]), (PSb[0:96, 3, :], stage[b][:, 384:480], ident[:])]
                                rd += [ST + "k", ST + "kr", "stagez%d" % b]
                            if lst:
                                tr(lst, rd, [PSr])
                            if full:
                                cp("dve", q_nT[:, :, ext_i * 128:(ext_i + 1) * 128], PSb[:, 0:2, :], [PSr], ["q_nT%d" % ext_i])
                            if kcol is not None:
                                cp("dve", kv_nT[:, kcol:kcol + 128], PSb[:, 2, :], [PSr], ["kv_nT"])
                                cp("dve", kT[64:96, kcol:kcol + 128], PSb[64:96, 3, :], [PSr], ["kTr"])

                        return (stA, stB, stC, stD)

                    def run_pipe(stage_lists):
                        n = len(stage_lists)
                        depth = len(stage_lists[0])
                        if not PIPE:
                            for sl in stage_lists:
                                for f_ in sl:
                                    f_()
                            return
                        for s_ in range(n + depth - 1):
                            for k in range(depth):
                                idx = s_ - k
                                if 0 <= idx < n:
                                    stage_lists[idx][k]()

                    tl = []
                    for i in range(10):
                        kcol = (i - 1) * 128 if (half == 0 and 1 <= i <= 8) else None
                        tl.append(p1_stages(len(tl), xext[seg, i * 128:(i + 1) * 128, :], True, kcol, rq[:, i, :], "rq", i))
                    if half == 0:
                        noth = NKT - 8
                        for j in range(noth):
                            b2 = j % 4
                            if gkind == "p":
                                srcx = xoth_p[gidx, j * 128:(j + 1) * 128, :]
                                srcr = ropek_p[gidx, j]
                            else:
                                srcx = xoth_s[j * 128:(j + 1) * 128, :]
                                srcr = ropek_s[j]

                            def pre(b2=b2, srcr=srcr):
                                dma("rk%d" % b2, rk[b2][:], srcr, [], ["rk%d" % b2])
                            tl.append(p1_stages(len(tl), srcx, False, 1024 + j * 128, rk[b2][:], "rk%d" % b2, None, pre))
                    run_pipe(tl)

                    def q_stages(i):
                        b = i % 2
                        PZ = P[b]
                        PZr = "P%d" % b
                        PTb = P[3][:, b * 512:(b + 1) * 512].bitcast(BF16).rearrange("p (a b) -> p a b", a=8)
                        PTr = "P3_%d" % b
                        cols = slice(i * 128, (i + 1) * 128)
                        QT = "qtm%d" % b

                        def q1():
                            lst = []
                            for c in range(2):
                                lst.append((PZ[:, 0:512], q_nT[:, c, cols], w_uq_b[:, c, 0:512], c == 0, c == 1))
                            for c in range(2):
                                lst.append((PZ[:, 512:768], q_nT[:, c, cols], w_uq_b[:, c, 512:768], c == 0, c == 1))
                            mm(lst, ["q_nT%d" % i, "w_uq_b"], [PZr])
                            q3 = PZ[:, 0:768].rearrange("p (h d) -> p h d", h=8)
                            o3 = qtm[b][:].rearrange("p (h d) -> p h d", h=8)
                            rt = rtmp[b]
                            RT = "rtmp%d" % b
                            acp(o3[:, :, 0:64], q3[:, :, 0:64], [PZr], [QT + "n"])
                            C2 = rq[:, i, 0:32].unsqueeze(1).to_broadcast([128, 8, 32])
                            Sa = rq[:, i, 32:48].unsqueeze(1).to_broadcast([128, 8, 16])
                            Sb = rq[:, i, 48:64].unsqueeze(1).to_broadcast([128, 8, 16])
                            tt("dve", rt[:, :, 0:32], q3[:, :, 64:96], C2, ALU.mult, [PZr, "rq"], [RT + "a"])
                            tt("dve", rt[:, :, 32:48], q3[:, :, 80:96], Sa, ALU.mult, [PZr, "rq"], [RT + "b"])
                            tt("dve", rt[:, :, 48:64], q3[:, :, 64:80], Sb, ALU.mult, [PZr, "rq"], [RT + "c"])
                            tt("dve", o3[:, :, 64:96], rt[:, :, 0:32], rt[:, :, 32:64], ALU.add,
                               [RT + "a", RT + "b", RT + "c"], [QT + "r"])

                        def q2():
                            tr([(PTb[0:96, h, :], qtm[b][:, h * 96:(h + 1) * 96], ident[:]) for h in range(8)],
                               [QT + "n", QT + "r", "ident"], [PTr])
                            if i == 0:
                                cp("dve", qT[:, :, 0:1], PTb[0:96, :, 127:128], [PTr], ["qT"])
                            elif i == 9:
                                cp("dve", qT[:, :, 1025:1026], PTb[0:96, :, 0:1], [PTr], ["qT"])
                            else:
                                cp("dve", qT[:, :, 1 + (i - 1) * 128:1 + i * 128], PTb[0:96, :, :], [PTr], ["qT"])

                        return (q1, q2)

                    run_pipe([q_stages(i) for i in range(10)])

                    cp("dve", band_fl[:], band_fl32[:], ["band_fl32"], ["band_fl"])

                    def pool_stages(items, ncols, qc, b):
                        PP = P[2][:, 0:512].rearrange("p (g t) -> p g t", g=4)
                        PM = P[2][:, 512:1024].rearrange("p (g t) -> p g t", g=4)

                        def pp1():
                            lst = []
                            rd = ["band_c", "band_fl"]
                            for g in range(4):
                                its = items[g]
                                for n_, (kt_, rhs) in enumerate(its):
                                    lst.append((PP[:, g, 0:ncols], upool[:, kt_, g * 128:(g + 1) * 128], rhs, n_ == 0, n_ == len(its) - 1))
                                    rd.append("upool%d" % kt_)
                            mm(lst, rd, ["P2_0"])
                            cp("dve", pooled[b][:, :, 0:ncols], PP[:, :, 0:ncols], ["P2_0"], ["pooled%d" % b])

                        def pp2():
                            mm([(PM[:, g, 0:ncols], w_pool_b[:, g, :], pooled[b][:, g, 0:ncols], True, True) for g in range(4)],
                               ["pooled%d" % b, "w_pool_b"], ["P2_1"])
                            acp(mixP[:, :, qc:qc + ncols], PM[:, :, 0:ncols], ["P2_1"], ["mixP"])

                        return (pp1, pp2)

                    pl = []
                    for i in range(1, 9):
                        items = []
                        for g in range(4):
                            if i == 1:
                                cur = band_fl[:, g, :]
                            elif i == 8:
                                cur = band_fl[:, 4 + g, :]
                            else:
                                cur = band_c[:, 4 + g, :]
                            items.append([(i - 1, band_c[:, g, :]), (i, cur), (i + 1, band_c[:, 8 + g, :])])
                        pl.append(pool_stages(items, 128, 1 + (i - 1) * 128, i % 2))
                    pl.append(pool_stages([[(0, band_c[:, 4 + g, 127:128]), (1, band_c[:, 8 + g, 127:128])] for g in range(4)], 1, 0, 1))
                    pl.append(pool_stages([[(8, band_c[:, g, 0:1]), (9, band_c[:, 4 + g, 0:1])] for g in range(4)], 1, 1025, 0))
                    run_pipe(pl)

                    bg_work = []
                    if seg == 0:
                        sg32 = [sb("sgA32_%d" % i, (128, 1024), F32, sa) for i in range(2)]
                        sg16 = [sb("sgA16_%d" % i, (128, 1024), BF16, sa) for i in range(2)]
                        bgc = [0]

                        chunks = []
                        for j in range(NJ):
                            for hh_ in range(2):
                                chunks.append((w_up_d[j][:, 4 * hh_:4 * hh_ + 4, :], s_w_up[j][:, 4 * hh_:4 * hh_ + 4, :],
                                               4, 256, gsm[:, 16 + 4 * hh_:20 + 4 * hh_]))
                        for j in range(11):
                            for hh_ in range(2):
                                chunks.append((w_down_d[j][:, 2 * hh_:2 * hh_ + 2, :], s_w_down[j][:, 2 * hh_:2 * hh_ + 2, :],
                                               2, 512, None))

                        def bg_load(i):
                            src, dst, a_, bdim, sc_ = chunks[i]
                            b_ = i % 2
                            v32 = sg32[b_][:, :].rearrange("p (a b) -> p a b", a=a_)
                            dma("bl32_%d" % b_, v32, src, [], ["sgA32_%d" % b_])

                        def bg_step(i):
                            def f_():
                                if i + 1 < len(chunks):
                                    bg_load(i + 1)
                                src, dst, a_, bdim, sc_ = chunks[i]
                                b_ = i % 2
                                v32 = sg32[b_][:, :].rearrange("p (a b) -> p a b", a=a_)
                                v16 = sg16[b_][:, :].rearrange("p (a b) -> p a b", a=a_)
                                if sc_ is None:
                                    cp("pool" if BG_POOL else "dve", v16, v32, ["sgA32_%d" % b_], ["sgA16_%d" % b_])
                                else:
                                    tt("pool" if BG_POOL else "dve", v16, v32, sc_.unsqueeze(2).to_broadcast([128, a_, bdim]), ALU.mult,
                                       ["sgA32_%d" % b_, "gsm"], ["sgA16_%d" % b_])
                                dma("bs16_%d" % b_, dst, v16, ["sgA16_%d" % b_], ["scr"])
                            return f_

                        bg_load(0)
                        bg_work = [bg_step(i) for i in range(len(chunks))]

                    P3a = P[3][:, 0:512]
                    PH = P[3][:, 512:576]
                    PO = P[2]
                    def head_prep(h):
                        odd = h % 2
                        o0 = 64 if odd else 0
                        meng = "dve" if seg == 0 else "pool"
                        if odd:
                            S.op(meng, lambda e, Vx=Vx: e.memset(Vx[:, 0:NKT, 0:64], 0.0), [], ["Vx"])
                            S.op(meng, lambda e, Vx=Vx: e.memset(Vx[:, 0:NKT, 0:1], 1.0), ["Vx"], ["Vx1"])
                        else:
                            S.op(meng, lambda e, Vx=Vx: e.memset(Vx[:, 0:NKT, 64:65], 1.0), ["Vx"], ["Vx1"])
                        rr = 0
                        for r in range(TK // 1024):
                            pb = P[rr % 2]
                            mm([(pb[0:64, 0:512], w_ukv_b[:, h * 128:h * 128 + 64], kv_nT[:, r * 1024:r * 1024 + 512], True, True),
                                (pb[0:64, 512:1024], w_ukv_b[:, h * 128:h * 128 + 64], kv_nT[:, r * 1024 + 512:r * 1024 + 1024], True, True)],
                               ["w_ukv_b", "kv_nT"], ["P%d" % (rr % 2)])
                            if rr % 2 == 1 or not PREP_ACT:
                                cp("dve", kT[0:64, r * 1024:(r + 1) * 1024], pb[0:64, :], ["P%d" % (rr % 2)], ["kTn"])
                            else:
                                act(kT[0:64, r * 1024:(r + 1) * 1024], pb[0:64, :], AF.Identity, ["P%d" % (rr % 2)], ["kTn"])
                            rr += 1
                        for r in range(NKT // 16):
                            pb = P[rr % 2]
                            mm([(pb[:, t16 * 64:(t16 + 1) * 64], kv_nT[:, (r * 16 + t16) * 128:(r * 16 + t16 + 1) * 128],
                                 w_ukv_b[:, h * 128 + 64:h * 128 + 128], True, True) for t16 in range(16)],
                               ["w_ukv_b", "kv_nT"], ["P%d" % (rr % 2)])
                            src3 = pb[:, :].rearrange("p (a b) -> p a b", a=16)
                            if rr % 2 == 1 or not PREP_ACT:
                                cp("dve", Vx[:, r * 16:(r + 1) * 16, o0:o0 + 64], src3, ["P%d" % (rr % 2), "Vx1"], ["Vx"])
                            else:
                                act(Vx[:, r * 16:(r + 1) * 16, o0:o0 + 64], src3, AF.Identity, ["P%d" % (rr % 2), "Vx1"], ["Vx"])
                            rr += 1

                    norm_sched = {}
                    if PREP_EARLY:
                        head_prep(0)
                    for h in range(8):
                        if not PREP_EARLY:
                            head_prep(h)
                        odd = h % 2
                        MO = 128 if odd else 65
                        o0 = 64 if odd else 0
                        sp = 0 if odd else 64
                        POH = P[3][0:MO, 640:642]

                        def s_op(kt):
                            b = kt % 2
                            hb = (kt // 16) % 2
                            j16 = kt % 16
                            kTt = kT[0:96, kt * 128:(kt + 1) * 128]
                            mm([(P[b][:, 0:512], kTt, qT[0:96, h, 1:513], True, True),
                                (P[b][:, 512:1024], kTt, qT[0:96, h, 513:1025], True, True),
                                (P[3][:, 512 + hb * 32 + j16 * 2:512 + hb * 32 + j16 * 2 + 2], kTt, qT[0:96, h, 0:1026:1025], True, True)],
                               ["kTn", "kTr", "qT"], ["P%d" % b, "PH%d_%d" % (hb, j16), "P3_1"])

                        def pv_op(kt):
                            p3 = kt % 3
                            hb = (kt // 16) % 2
                            mm([(PO[0:MO, 0:512], Vx[:, kt, 0:MO], PTs[p3][:, 0:512], kt == 0, kt == NKT - 1),
                                (PO[0:MO, 512:1024], Vx[:, kt, 0:MO], PTs[p3][:, 512:1024], kt == 0, kt == NKT - 1)],
                               ["Vx", "Vx1", "PTs%d" % p3], ["PO", "P2_0", "P2_1"])
                            if kt % 16 == 15:
                                act(PHs[hb][:], P[3][:, 512 + hb * 32:512 + hb * 32 + 32], AF.Exp,
                                    ["PH%d_%d" % (hb, jj) for jj in range(16)] + ["P3_1"], ["PHs%d" % hb], scale=SC)
                                k0 = kt - 15
                                mm([(POH, Vx[:, k0 + jj, 0:MO], PHs[hb][:, jj * 2:jj * 2 + 2], jj == 0, jj == 15)
                                    for jj in range(16)], ["Vx", "Vx1", "PHs%d" % hb], ["POH", "P3_1"])
                                if k0 == 0:
                                    cp("dve", ohacc[0:MO, :], POH, ["POH"], ["ohacc"])
                                else:
                                    tt("dve", ohacc[0:MO, :], POH, ohacc[0:MO, :], ALU.add, ["POH", "ohacc"], ["ohacc"])

                        s_op(0)
                        for kt in range(NKT):
                            b = kt % 2
                            if kt + 1 < NKT:
                                s_op(kt + 1)
                            act(PTs[kt % 3][:], P[b][:], AF.Exp, ["P%d" % b], ["PTs%d" % (kt % 3)], scale=SC)
                            if kt >= 1:
                                pv_op(kt - 1)
                            if kt in norm_sched:
                                norm_sched.pop(kt)()
                            if bg_work:
                                bg_work.pop(0)()
                        pv_op(NKT - 1)
                        if PREP_EARLY and h + 1 < 8:
                            head_prep(h + 1)
                        for c in range(2):
                            cp("dve", otmp[c][0:MO, :], PO[0:MO, c * 512:(c + 1) * 512], ["PO"], ["otmp%d" % c])
                        cp("dve", oth[0:MO, :], ohacc[0:MO, :], ["ohacc"], ["oth"])

                        def mk_norm(h=h, MO=MO, o0=o0, sp=sp):
                            def n1():
                                for c in range(2):
                                    S.op("dve", lambda e, o=rec[c][sp:sp + 1, :], i_=otmp[c][sp:sp + 1, :]: e.reciprocal(out=o, in_=i_),
                                         ["otmp%d" % c], ["rec%d" % c, "recb%d" % c])
                                S.op("dve", lambda e, o=rech[sp:sp + 1, :], i_=oth[sp:sp + 1, :]: e.reciprocal(out=o, in_=i_),
                                     ["oth"], ["rech", "rechb"])

                            def n2():
                                for c in range(2):
                                    dma("nw%d" % c, recd[c:c + 1, :], rec[c][sp:sp + 1, :], ["rec%d" % c], ["recd%d" % c])
                                dma("nwh", recd[2:3, 0:2], rech[sp:sp + 1, :], ["rech"], ["recd2"])
                                for c in range(2):
                                    dma("nb%d" % c, rec[c][o0:o0 + 64, :], recd[c:c + 1, :].to_broadcast([64, 512]),
                                        ["recd%d" % c], ["recb%d" % c])
                                dma("nbh", rech[o0:o0 + 64, :], recd[2:3, 0:2].to_broadcast([64, 2]), ["recd2"], ["rechb"])

                            def n3(c):
                                def f_():
                                    tt("dve", attnT[o0:o0 + 64, h // 2, 1 + c * 512:1 + (c + 1) * 512], otmp[c][o0:o0 + 64, :],
                                       rec[c][o0:o0 + 64, :], ALU.mult, ["otmp%d" % c, "recb%d" % c], ["attnT"])
                                return f_

                            def n4():
                                tt("dve", attnT[o0:o0 + 64, h // 2, 0:1026:1025], oth[o0:o0 + 64, :], rech[o0:o0 + 64, :], ALU.mult,
                                   ["oth", "rechb"], ["attnT"])
                            return {0: n1, 4: n2, 9: n3(0), 11: n3(1), 13: n4}

                        norm_sched = mk_norm()
                    for k_ in sorted(norm_sched):
                        norm_sched[k_]()
                    while bg_work:
                        bg_work.pop(0)()
                    with nc.Block() as blk:
                        S.flush(blk)

                with ExitStack() as sbk:
                    xm = sb("xm", (128, 10, D), F32, sbk)
                    h2T = sb("h2T", (128, 8, 1026), BF16, sbk)
                    actT = sb("actT", (128, NJ, 512), BF16, sbk)
                    wu = [sb("wu%d" % i, (128, 8, 256), BF16, sbk) for i in range(2)]
                    wd = [sb("wd%d" % i, (128, 4, 512), BF16, sbk) for i in range(2)]
                    ga = [sb("ga%d" % i, (128, 512), F32, sbk) for i in range(2)]
                    va = [sb("va%d" % i, (128, 512), F32, sbk) for i in range(2)]
                    sg = [sb("sg%d" % i, (128, 512), F32, sbk) for i in range(2)]
                    tv = [sb("tv%d" % i, (128, 512), F32, sbk) for i in range(2)]
                    xtb = [sb("xtb%d" % i, (128, D), F32, sbk) for i in range(2)]
                    xn2 = [sb("xn2_%d" % i, (128, D), BF16, sbk) for i in range(2)]
                    junkb = sb("junkB", (128, D), BF16, sbk)
                    yo = [sb("yo%d" % i, (128, D), F32, sbk) for i in range(2)]

                    tiles = [(127, 1, 0, 0)] + [(128 * m, 128, 1 + (m - 1) * 128, m) for m in range(1, 9)] + [(1152, 1, 1025, 9)]

                    def xm_stages(ti, row, pp, qc, xi):
                        b = ti % 2
                        PZ = P[b]
                        PZr = "P%d" % b
                        PTb = P[3][:, b * 512:(b + 1) * 512].bitcast(BF16).rearrange("p (a b) -> p a b", a=8)
                        PTr = "P3_%d" % b
                        XB, XN2 = "xtb%d" % b, "xn2_%d" % b

                        def x1():
                            dma("xb%d" % b, xtb[b][0:pp, :], xext[seg, row:row + pp, :], [], [XB])
                            for hf in range(2):
                                lst = []
                                for g in range(4):
                                    lst.append((PZ[0:pp, hf * 512:(hf + 1) * 512], mixP[:, g, qc:qc + pp],
                                                w_out_b[:, g, hf * 512:(hf + 1) * 512], g == 0, False))
                                for hp in range(4):
                                    lst.append((PZ[0:pp, hf * 512:(hf + 1) * 512], attnT[:, hp, qc:qc + pp],
                                                w_out_b[:, 4 + hp, hf * 512:(hf + 1) * 512], False, hp == 3))
                                mm(lst, ["mixP", "attnT", "w_out_b"], [PZr])
                            tt("dve", xm[0:pp, xi, :], PZ[0:pp, :], xtb[b][0:pp, :], ALU.add, [PZr, XB], ["xm%d" % xi])
                            rs = rms_stats(xm[0:pp, xi, :], D, pp, 8 + 4 * b, "xm%d" % xi, "n2_%d" % b, junkb[0:pp, :])
                            ts("dve", xn2[b][0:pp, :], xm[0:pp, xi, :], rs, None, ALU.mult, None, ["xm%d" % xi, "n2_%drs" % b], [XN2])

                        def x2():
                            tr([(PTb[:, c, 0:pp], xn2[b][0:pp, c * 128:(c + 1) * 128], ident[0:pp, 0:pp]) for c in range(8)],
                               [XN2, "ident"], [PTr])
                            if pp == 1:
                                side = 0 if xi == 0 else 1
                                ts("dve", h2T[:, :, qc:qc + 1], PTb[:, :, 0:1], masks[:, seg * 2 + side:seg * 2 + side + 1], None,
                                   ALU.mult, None, [PTr, "masks"], ["h2T"])
                            else:
                                cp("dve", h2T[:, :, qc:qc + 128], PTb, [PTr], ["h2T"])

                        return (x1, x2)

                    xl = [xm_stages(ti, *t_) for ti, t_ in enumerate(tiles)]
                    if PIPE:
                        for s_ in range(len(xl) + 1):
                            if s_ < len(xl):
                                xl[s_][0]()
                            if s_ >= 1:
                                xl[s_ - 1][1]()
                    else:
                        for x1_, x2_ in xl:
                            x1_()
                            x2_()

                    cpv = convp[:].rearrange("p (j s k) -> p j s k", j=NJ, s=2)
                    ucount = 0
                    deferred = []
                    for ck in range(2):
                        for j in range(NJ):
                            wb_ = j % 2
                            if not (ck == 1 and j < 2):
                                dma("wu%d" % wb_, wu[wb_][:], s_w_up[j], ["scr"], ["wu%d" % wb_])
                            b = ucount % 2
                            ucount += 1
                            gi_, vi_ = 2 * b, 2 * b + 1
                            PG, PV_ = P[gi_], P[vi_]
                            for sub in range(2):
                                c0 = ck * 512 + sub * 256
                                lst = []
                                for c in range(8):
                                    lst.append((PG[:, sub * 512:sub * 512 + 258], wu[wb_][:, c, 0:128], h2T[:, c, c0:c0 + 258], c == 0, c == 7))
                                for c in range(8):
                                    lst.append((PV_[:, sub * 512:sub * 512 + 258], wu[wb_][:, c, 128:256], h2T[:, c, c0:c0 + 258], c == 0, c == 7))
                                mm(lst, ["wu%d" % wb_, "h2T"], ["P%d%s" % (gi_, "ab"[sub]), "P%d%s" % (vi_, "ab"[sub])])
                            G3 = PG[:, :].rearrange("p (s c) -> p s c", s=2)
                            V3 = PV_[:, :].rearrange("p (s c) -> p s c", s=2)
                            ga3 = ga[b][:, :].rearrange("p (s c) -> p s c", s=2)
                            va3 = va[b][:, :].rearrange("p (s c) -> p s c", s=2)
                            for s_, (acc, nm, src3_, pr_) in enumerate(((ga3, "ga%d" % b, G3, "P%d" % gi_), (va3, "va%d" % b, V3, "P%d" % vi_))):
                                act(acc, src3_[:, :, 0:256], AF.Identity, [pr_, "convp"], [nm],
                                    scale=cpv[:, j, s_, 0:1], bias=cpv[:, j, s_, 3:4])
                            tv3 = tv[b][:, :].rearrange("p (s c) -> p s c", s=2)
                            act(tv3, V3[:, :, 2:258], AF.Identity, ["P%d" % vi_, "convp"], ["tv%d" % b], scale=cpv[:, j, 1, 2:3])
                            stt("dve", ga3, G3[:, :, 1:257], cpv[:, j, 0, 1:2], ga3, ALU.mult, ALU.add,
                                ["P%d" % gi_, "convp", "ga%d" % b], ["ga%d" % b])
                            stt("dve", ga3, G3[:, :, 2:258], cpv[:, j, 0, 2:3], ga3, ALU.mult, ALU.add,
                                ["P%d" % gi_, "convp", "ga%d" % b], ["ga%d" % b])
                            stt("dve", va3, V3[:, :, 1:257], cpv[:, j, 1, 1:2], va3, ALU.mult, ALU.add,
                                ["P%d" % vi_, "convp", "va%d" % b], ["va%d" % b])
                            act(sg[b][:], ga[b][:], AF.Silu, ["ga%d" % b], ["sg%d" % b])
                            tt("pool", va[b][:], va[b][:], tv[b][:], ALU.add, ["va%d" % b, "tv%d" % b], ["va%d" % b])
                            tt("pool", actT[:, j, :], sg[b][:], va[b][:], ALU.mult,
                               ["sg%d" % b, "va%d" % b], ["actT%d" % j])
                            if j == 18:
                                for jg_ in range(2):
                                    j0_, nj_ = ((0, 4), (4, 4))[jg_]
                                    dma("wd%d" % jg_, wd[jg_][:, 0:nj_, :], s_w_down[j0_ // 4][:, 0:nj_, :], ["scr"], ["wd%d" % jg_])
                            if deferred and j >= 2 and j % 2 == 0:
                                deferred.pop(0)()
                        if ck == 0:
                            for j_ in range(2):
                                dma("wu%d" % j_, wu[j_][:], s_w_up[j_], ["scr"], ["wu%d" % j_])
                        for hf in range(2):
                            grp = [(0, 4), (4, 4), (8, 4), (12, 4), (16, 4), (20, 2)] if hf == 0 else \
                                  [(0, 2), (2, 4), (6, 4), (10, 4), (14, 4), (18, 4)]
                            for jg, (j0, nj) in enumerate(grp):
                                wb_ = jg % 2
                                idx0 = hf * NJ + j0
                                if not (hf == 0 and jg < 2):
                                    dma("wd%d" % wb_, wd[wb_][:, 0:nj, :], s_w_down[idx0 // 4][:, idx0 % 4:idx0 % 4 + nj, :],
                                        ["scr"], ["wd%d" % wb_])
                                for jj in range(nj):
                                    j = j0 + jj
                                    lst = []
                                    for m in range(4):
                                        lst.append((P[2 * hf + m // 2][:, (m % 2) * 512:(m % 2) * 512 + 512],
                                                    actT[:, j, m * 128:(m + 1) * 128], wd[wb_][:, jj, :], j == 0, j == NJ - 1))
                                    mm(lst, ["actT%d" % j, "wd%d" % wb_], ["P%d" % (2 * hf), "P%d" % (2 * hf + 1)])
                            for m in range(4):
                                xi = 1 + ck * 4 + m
                                tt("dve", xm[:, xi, hf * 512:(hf + 1) * 512], P[2 * hf + m // 2][:, (m % 2) * 512:(m % 2) * 512 + 512],
                                   xm[:, xi, hf * 512:(hf + 1) * 512], ALU.add, ["P%d%s" % (2 * hf + m // 2, "ab"[m % 2]), "xm%d" % xi], ["xm%d" % xi])
                        def mk_fin(xi, ob):
                            def f_():
                                rs = rms_stats(xm[:, xi, :], D, 128, 4, "xm%d" % xi, "n3", junkb[:])
                                stt("dve", yo[ob][:], xm[:, xi, :], rs, gfin[:], ALU.mult, ALU.mult,
                                    ["xm%d" % xi, "n3rs", "gfin"], ["yo%d" % ob])
                                dma("yo%d" % ob, y_out[seg, (xi - 1) * 128:xi * 128, :], yo[ob][:], ["yo%d" % ob], ["yout"])
                            return f_

                        fins = [mk_fin(1 + ck * 4 + m, m % 2) for m in range(4)]
                        for f_ in fins:
                            f_()
                    with nc.Block() as blk:
                        S.flush(blk)
                        if seg == NSEG - 1:
                            pass
        with nc.Block() as blk:
            S.final_wait(blk)
    return nc


def _rope_tab(pos):
    inv = (10000.0 ** (-np.arange(0, 32, 2, dtype=np.float32) / 32.0)).astype(np.float32)
    ang = pos.astype(np.float32)[:, None] * inv[None, :]
    c, s_ = np.cos(ang).astype(np.float32), np.sin(ang).astype(np.float32)
    return np.concatenate([c, c, -s_, s_], axis=1).astype(np.float32)


def _band(kind):
    wins = (2, 4, 8, 16)
    if kind == "c":
        out = np.zeros((3, 4, 128, 128), np.float32)
        for g, w in enumerate(wins):
            hw = w // 2
            for t in range(128):
                ta = 128 + t
                for tp in range(ta - hw, ta + hw):
                    out[tp // 128, g, tp % 128, t] += 1.0 / w
                out[1, g, t, t] -= 1.0
        return out
    out = np.zeros((4, 128, 128), np.float32)
    for g, w in enumerate(wins):
        hw = w // 2
        for t in range(128):
            lo, hi = t - hw, t + hw
            if kind == "f":
                lo = max(lo, 0)
            else:
                hi = min(hi, 128)
            cnt = hi - lo
            for tp in range(max(lo, 0), min(hi, 128)):
                out[g, tp, t] += 1.0 / cnt
            out[g, t, t] -= 1.0
    return out


_NC_CACHE = {}


def kernel(x_prompt, x_sample, norm_mix_g, w_in, q_norm_g, w_uq, kv_norm_g, w_ukv, w_pool, pool_scale,
           w_out, norm_ffn_g, w_up, conv_w, conv_b, w_down, final_norm_g):
    f = np.float32
    x_prompt = np.asarray(x_prompt, f)
    x_sample = np.asarray(x_sample, f)

    def pk(w, k):
        w = np.asarray(w, f)
        return np.ascontiguousarray(w.reshape(k, 128, -1).transpose(1, 0, 2))

    def pv(v, k):
        return np.ascontiguousarray(np.asarray(v, f).reshape(k, 128).T)

    w_up0 = np.asarray(w_up, f)[0]
    wu = np.empty((NJ, 128, 8, 256), f)
    for j in range(NJ):
        wu[j, :, :, 0:128] = w_up0[:, j * 128:(j + 1) * 128].reshape(8, 128, 128).transpose(1, 0, 2)
        wu[j, :, :, 128:256] = w_up0[:, DFF + j * 128:DFF + (j + 1) * 128].reshape(8, 128, 128).transpose(1, 0, 2)
    w_down0 = np.asarray(w_down, f)[0]
    wdn = np.empty((11, 128, 4, 512), f)
    for hf in range(2):
        for j in range(NJ):
            idx = hf * NJ + j
            wdn[idx // 4, :, idx % 4, :] = w_down0[j * 128:(j + 1) * 128, hf * 512:(hf + 1) * 512]
    cw = np.asarray(conv_w, f)[0]
    cb = np.asarray(conv_b, f)[0]
    convp = np.empty((128, NJ, 2, 4), f)
    for j in range(NJ):
        for s_, off in enumerate((0, DFF)):
            sl = slice(off + j * 128, off + (j + 1) * 128)
            convp[:, j, s_, 0] = cw[0, sl]
            convp[:, j, s_, 1] = cw[1, sl]
            convp[:, j, s_, 2] = cw[2, sl]
            convp[:, j, s_, 3] = cb[sl]
    bc = _band("c")
    band_c = np.ascontiguousarray(bc.reshape(12, 128, 128).transpose(1, 0, 2))
    bcur = bc[1]
    bfirst, blast = _band("f"), _band("l")
    shared = {
        "band_c": band_c,
        "w_in_h": pk(np.asarray(w_in, f)[0], 8), "g_mix": pv(np.asarray(norm_mix_g, f)[0], 8),
        "w_uq_h": pk(np.asarray(w_uq, f)[0], 2), "g_q": pv(np.asarray(q_norm_g, f)[0], 2),
        "w_ukv_h": pk(np.asarray(w_ukv, f)[0], 1), "g_kv": pv(np.asarray(kv_norm_g, f)[0], 1),
        "w_pool_h": np.ascontiguousarray(np.asarray(w_pool, f)[0].transpose(1, 0, 2)),
        "w_out_h": pk(np.asarray(w_out, f)[0], 8), "pscale": pv(np.asarray(pool_scale, f)[0], 4),
        "w_up_h": wu, "g_ffn": pv(np.asarray(norm_ffn_g, f)[0], 8), "w_down_h": wdn,
        "convp": np.ascontiguousarray(convp.reshape(128, NJ * 8)),
        "gfin": np.ascontiguousarray(np.broadcast_to(np.asarray(final_norm_g, f)[None, :], (128, D))),
        "ident": np.eye(128, dtype=f),
    }

    def ext_rows(seq, s0):
        S_ = seq.shape[0]
        out = np.zeros((EXT, D), f)
        lo, hi = s0 - 128, s0 + 1024 + 128
        a, b = max(lo, 0), min(hi, S_)
        out[a - lo:b - lo] = seq[a:b]
        return out

    in_maps = []
    seginfo = []
    for c in range(8):
        groups = [(x_prompt[2 * c], 0), (x_prompt[2 * c + 1], 0), (x_sample[c // 4], (c % 4) * 2048)]
        xext = np.empty((NSEG, EXT, D), f)
        ropeq = np.empty((NSEG, 128, 10, 64), f)
        masks = np.zeros((NSEG, 2), f)
        band_f = np.empty((NSEG, 128, 4, 128), f)
        band_l = np.empty((NSEG, 128, 4, 128), f)
        for gi, (seq, gs) in enumerate(groups):
            S_ = seq.shape[0]
            for half in range(2):
                seg = gi * 2 + half
                s0 = gs + half * 1024
                xext[seg] = ext_rows(seq, s0)
                pos = np.arange(s0 - 128, s0 + 1152)
                ropeq[seg] = _rope_tab(pos).reshape(10, 128, 64).transpose(1, 0, 2)
                masks[seg, 0] = 1.0 if s0 - 1 >= 0 else 0.0
                masks[seg, 1] = 1.0 if s0 + 1024 < S_ else 0.0
                band_f[seg] = (bfirst if s0 == 0 else bcur).transpose(1, 0, 2)
                band_l[seg] = (blast if s0 + 1024 == S_ else bcur).transpose(1, 0, 2)
        xoth_p = np.stack([x_prompt[2 * c][1024:2048], x_prompt[2 * c + 1][1024:2048]])
        ropek_p = np.stack([_rope_tab(np.arange(1024, 2048)).reshape(8, 128, 64)] * 2)
        xs = x_sample[c // 4]
        gs = (c % 4) * 2048
        opos = np.concatenate([np.arange(gs + 1024, gs + 2048), np.arange(0, gs), np.arange(gs + 2048, 8192)])
        xoth_s = np.ascontiguousarray(xs[opos])
        ropek_s = _rope_tab(opos).reshape(56, 128, 64)
        m = dict(shared)
        m.update({
            "xext": xext, "xoth_p": np.ascontiguousarray(xoth_p), "xoth_s": xoth_s,
            "ropeq": ropeq, "ropek_p": np.ascontiguousarray(ropek_p), "ropek_s": np.ascontiguousarray(ropek_s),
            "masks": np.ascontiguousarray(np.broadcast_to(masks.reshape(1, 12), (128, 12))),
            "band_f": band_f, "band_l": band_l,
        })
        in_maps.append(m)

    if "nc" not in _NC_CACHE:
        _NC_CACHE["nc"] = build_program()
    nc = _NC_CACHE["nc"]
    res = run_bass_kernel_spmd(nc, in_maps, core_ids=list(range(8)))
    y_prompt = np.empty((16, 2048, D), f)
    y_sample = np.empty((2, 8192, D), f)
    for c in range(8):
        y = np.asarray(res.results[c]["y"], f).reshape(NSEG, 1024, D)
        y_prompt[2 * c] = y[0:2].reshape(2048, D)
        y_prompt[2 * c + 1] = y[2:4].reshape(2048, D)
        gs = (c % 4) * 2048
        y_sample[c // 4, gs:gs + 2048] = y[4:6].reshape(2048, D)
    return (y_prompt, y_sample)
```
